# Optimizing a Trainium2 kernel written in Bass

```python
import jax, jax.numpy as jnp
from jax import lax
import numpy as np

D_MODEL = 1024
BATCH = 4
SEQ = 4096
DEPTH = 1

D_MIX = D_MODEL
D_FOX = D_MIX // 2
D_RET = D_MIX - D_FOX
FOX_HEADS = 8
FOX_HEAD_DIM = D_FOX // FOX_HEADS
RET_HEADS = 4
RET_HEAD_DIM = D_RET // RET_HEADS
D_FF = 2816
BLOCK_Q = 128
RET_CHUNK = 128
ROPE_BASE = 10000.0
LN_EPS = 1e-5
N_MOD = 9
DEEPNORM_ALPHA = (2.0 * DEPTH) ** 0.25
DEEPNORM_BETA = (8.0 * DEPTH) ** -0.25
FFN_RES_WEIGHT = 0.5
SPLITS = [D_FOX, 2 * D_FOX, 3 * D_FOX, 3 * D_FOX + FOX_HEADS,
          3 * D_FOX + FOX_HEADS + D_RET, 3 * D_FOX + FOX_HEADS + 2 * D_RET,
          3 * D_FOX + FOX_HEADS + 3 * D_RET]
D_IN_PROJ = 3 * D_FOX + FOX_HEADS + 4 * D_RET

kernel_name = "fox_retnet_hymba_macaron_deepnorm_adaln"


def _layer_norm(x, g, b):
    xf = x.astype(jnp.float32)
    mu = xf.mean(-1, keepdims=True)
    var = jnp.square(xf - mu).mean(-1, keepdims=True)
    return ((xf - mu) * lax.rsqrt(var + LN_EPS)).astype(x.dtype) * g + b


def _modulate(x, shift, scale):
    return x * (1.0 + scale[:, None, :]) + shift[:, None, :]


def _swiglu(h, w_gate, w_up, w_down):
    return (jax.nn.silu(h @ w_gate) * (h @ w_up)) @ w_down


def _heads(t, n_heads):
    B, S, _ = t.shape
    return t.reshape(B, S, n_heads, -1).transpose(0, 2, 1, 3)


def _merge_heads(t):
    B, H, S, Dh = t.shape
    return t.transpose(0, 2, 1, 3).reshape(B, S, H * Dh)


def _rotary(t):
    S, Dk = t.shape[2], t.shape[3]
    half = Dk // 2
    inv_freq = ROPE_BASE ** (-jnp.arange(half, dtype=jnp.float32) / half)
    ang = jnp.arange(S, dtype=jnp.float32)[:, None] * inv_freq[None, :]
    cos = jnp.cos(ang).astype(t.dtype)
    sin = jnp.sin(ang).astype(t.dtype)
    t1, t2 = t[..., :half], t[..., half:]
    return jnp.concatenate([t1 * cos - t2 * sin, t1 * sin + t2 * cos], axis=-1)


def _forgetting_attention(q, k, v, log_f):
    B, H, S, Dh = q.shape
    cum = jnp.cumsum(log_f, axis=-1)
    scale = Dh ** -0.5
    kpos = jnp.arange(S)
    n_blocks = S // BLOCK_Q

    def one_block(i):
        start = i * BLOCK_Q
        qb = lax.dynamic_slice_in_dim(q, start, BLOCK_Q, axis=2)
        cb = lax.dynamic_slice_in_dim(cum, start, BLOCK_Q, axis=2)
        s = jnp.einsum('bhqd,bhkd->bhqk', qb, k).astype(jnp.float32) * scale
        s = s + cb[..., :, None] - cum[..., None, :]
        qpos = start + jnp.arange(BLOCK_Q)
        s = jnp.where(kpos[None, :] <= qpos[:, None], s, -jnp.inf)
        p = jax.nn.softmax(s, axis=-1).astype(v.dtype)
        return jnp.einsum('bhqk,bhkd->bhqd', p, v)

    out = lax.map(one_block, jnp.arange(n_blocks))
    return out.transpose(1, 2, 0, 3, 4).reshape(B, H, S, Dh)


def _retention_chunkwise(q, k, v):
    B, H, S, Dk = q.shape
    Dv = v.shape[-1]
    C = RET_CHUNK
    n = S // C
    log_gamma = jnp.log1p(-jnp.power(2.0, -5.0 - jnp.arange(H, dtype=jnp.float32)))
    idx = jnp.arange(C, dtype=jnp.float32)
    diff = idx[:, None] - idx[None, :]
    intra_decay = jnp.where(diff >= 0,
                            jnp.exp(log_gamma[:, None, None] * jnp.maximum(diff, 0.0)), 0.0)
    q_decay = jnp.exp(log_gamma[:, None] * (idx + 1.0))[..., None]
    k_decay = jnp.exp(log_gamma[:, None] * (C - 1.0 - idx))[..., None]
    chunk_decay = jnp.exp(log_gamma * C)[:, None, None]

    def to_chunks(t):
        return t.reshape(B, H, n, C, t.shape[-1]).transpose(2, 0, 1, 3, 4)

    def step(state, inp):
        qi, ki, vi = inp
        s = jnp.einsum('bhid,bhjd->bhij', qi, ki) * intra_decay
        o = jnp.einsum('bhij,bhjv->bhiv', s, vi) + jnp.einsum('bhid,bhdv->bhiv', qi * q_decay, state)
        new_state = state * chunk_decay + jnp.einsum('bhjd,bhjv->bhdv', ki * k_decay, vi)
        return new_state, o

    state0 = jnp.zeros((B, H, Dk, Dv), jnp.float32)
    _, out = lax.scan(step, state0, (to_chunks(q), to_chunks(k), to_chunks(v)))
    return out.transpose(1, 2, 0, 3, 4).reshape(B, H, S, Dv).astype(v.dtype)


def _group_norm_heads(y, g, b):
    yf = y.astype(jnp.float32)
    mu = yf.mean(-1, keepdims=True)
    var = jnp.square(yf - mu).mean(-1, keepdims=True)
    yn = ((yf - mu) * lax.rsqrt(var + LN_EPS)).astype(y.dtype)
    return _merge_heads(yn) * g + b


def _hybrid_mixer(h, w_in, fox_b_f, ret_gn_g, ret_gn_b, w_o):
    proj = h @ w_in
    fq, fk, fv, fl, rq, rk, rv, rg = jnp.split(proj, SPLITS, axis=-1)
    log_f = jax.nn.log_sigmoid(fl.astype(jnp.float32) + fox_b_f.astype(jnp.float32))
    log_f = log_f.transpose(0, 2, 1)
    fox = _forgetting_attention(_heads(fq, FOX_HEADS), _heads(fk, FOX_HEADS),
                                _heads(fv, FOX_HEADS), log_f)
    fox = _merge_heads(fox)
    q_r = _rotary(_heads(rq, RET_HEADS))
    k_r = _rotary(_heads(rk, RET_HEADS)) * (RET_HEAD_DIM ** -0.5)
    ret = _retention_chunkwise(q_r, k_r, _heads(rv, RET_HEADS))
    ret = jax.nn.silu(rg) * _group_norm_heads(ret, ret_gn_g, ret_gn_b)
    return jnp.concatenate([fox, ret], axis=-1) @ w_o


def setup_inputs(seed: int = 0) -> dict:
    key = jax.random.key(seed)
    ks = jax.random.split(key, 24)
    f32 = jnp.float32
    L, D = DEPTH, D_MODEL

    def nrm(k, shape, scale):
        return jax.random.normal(k, shape, f32) * scale

    x = jax.random.normal(ks[0], (BATCH, SEQ, D), f32)
    c = jax.random.normal(ks[1], (BATCH, D), f32)
    w_ada = nrm(ks[2], (L, D, N_MOD * D), 0.5 * D ** -0.5)
    b_ada = nrm(ks[3], (L, N_MOD * D), 0.02)

    def ffn(k):
        k1, k2, k3 = jax.random.split(k, 3)
        return (nrm(k1, (L, D, D_FF), D ** -0.5),
                nrm(k2, (L, D, D_FF), D ** -0.5),
                nrm(k3, (L, D_FF, D), DEEPNORM_BETA * D_FF ** -0.5))

    ffn1_w_gate, ffn1_w_up, ffn1_w_down = ffn(ks[4])
    ffn2_w_gate, ffn2_w_up, ffn2_w_down = ffn(ks[5])

    def ln(k):
        k1, k2 = jax.random.split(k)
        return 1.0 + nrm(k1, (L, D), 0.02), nrm(k2, (L, D), 0.02)

    ln1_g, ln1_b = ln(ks[6])
    ln2_g, ln2_b = ln(ks[7])
    ln3_g, ln3_b = ln(ks[8])

    s_in = D ** -0.5
    w_in = jnp.concatenate([
        nrm(ks[9], (L, D, D_FOX), s_in),
        nrm(ks[10], (L, D, D_FOX), s_in),
        nrm(ks[11], (L, D, D_FOX), DEEPNORM_BETA * s_in),
        nrm(ks[12], (L, D, FOX_HEADS), s_in),
        nrm(ks[13], (L, D, D_RET), s_in),
        nrm(ks[14], (L, D, D_RET), s_in),
        nrm(ks[15], (L, D, D_RET), DEEPNORM_BETA * s_in),
        nrm(ks[16], (L, D, D_RET), s_in),
    ], axis=-1)
    fox_b_f = 1.0 + 2.0 * jax.random.uniform(ks[17], (L, FOX_HEADS), f32)
    ret_gn_g = 1.0 + nrm(ks[18], (L, D_RET), 0.02)
    ret_gn_b = nrm(ks[19], (L, D_RET), 0.02)
    w_o = nrm(ks[20], (L, D_MIX, D), DEEPNORM_BETA * D_MIX ** -0.5)

    return {"x": x, "c": c, "w_ada": w_ada, "b_ada": b_ada,
            "ffn1_w_gate": ffn1_w_gate, "ffn1_w_up": ffn1_w_up, "ffn1_w_down": ffn1_w_down,
            "ln1_g": ln1_g, "ln1_b": ln1_b,
            "w_in": w_in, "fox_b_f": fox_b_f, "ret_gn_g": ret_gn_g, "ret_gn_b": ret_gn_b,
            "w_o": w_o, "ln2_g": ln2_g, "ln2_b": ln2_b,
            "ffn2_w_gate": ffn2_w_gate, "ffn2_w_up": ffn2_w_up, "ffn2_w_down": ffn2_w_down,
            "ln3_g": ln3_g, "ln3_b": ln3_b}


def reference(x, c, w_ada, b_ada,
              ffn1_w_gate, ffn1_w_up, ffn1_w_down, ln1_g, ln1_b,
              w_in, fox_b_f, ret_gn_g, ret_gn_b, w_o, ln2_g, ln2_b,
              ffn2_w_gate, ffn2_w_up, ffn2_w_down, ln3_g, ln3_b):
    c_act = jax.nn.silu(c)
    for l in range(DEPTH):
        mod = c_act @ w_ada[l] + b_ada[l]
        sh1, sc1, g1, sh2, sc2, g2, sh3, sc3, g3 = jnp.split(mod, N_MOD, axis=-1)
        h = _modulate(x, sh1, sc1)
        f = _swiglu(h, ffn1_w_gate[l], ffn1_w_up[l], ffn1_w_down[l])
        x = _layer_norm(DEEPNORM_ALPHA * x + FFN_RES_WEIGHT * g1[:, None, :] * f, ln1_g[l], ln1_b[l])
        h = _modulate(x, sh2, sc2)
        m = _hybrid_mixer(h, w_in[l], fox_b_f[l], ret_gn_g[l], ret_gn_b[l], w_o[l])
        x = _layer_norm(DEEPNORM_ALPHA * x + g2[:, None, :] * m, ln2_g[l], ln2_b[l])
        h = _modulate(x, sh3, sc3)
        f = _swiglu(h, ffn2_w_gate[l], ffn2_w_up[l], ffn2_w_down[l])
        x = _layer_norm(DEEPNORM_ALPHA * x + FFN_RES_WEIGHT * g3[:, None, :] * f, ln3_g[l], ln3_b[l])
    return x
```

```python
import contextlib
import numpy as np
import concourse.bass as bass
import concourse.mybir as mybir
from concourse.bass_utils import run_bass_kernel_spmd

F32 = mybir.dt.float32
BF16 = mybir.dt.bfloat16
AF = mybir.ActivationFunctionType
ALU = mybir.AluOpType

D = 1024
DFF = 2816
NT = 2048
ALPHA = 2.0 ** 0.25
EPS = 1e-5
NEG = -30000.0
RET_H = 4
GAMMA = [float(np.exp(np.float32(np.log1p(-np.float32(2.0) ** np.float32(-5.0 - h))))) for h in range(RET_H)]
DEBUG = None


class Sched:
    def __init__(self, nc):
        self.nc = nc
        self.eng = {"pe": nc.tensor, "act": nc.scalar, "dve": nc.vector, "pool": nc.gpsimd, "sp": nc.sync}
        self.sem = {e: nc.alloc_semaphore("s_" + e) for e in ("pe", "act", "dve", "pool")}
        self.cnt = {e: 0 for e in self.sem}
        self.dsem = {}
        self.dcnt = {}
        self.lastw = {}
        self.readers = {}
        self.waited = {e: {} for e in self.eng}

    def _semof(self, tok):
        return self.sem[tok[0]] if tok[0] in self.sem else self.dsem[tok[0]]

    @staticmethod
    def _excl(rd, wr):
        ps = [k for k in rd if isinstance(k, tuple) and k[0] == "ps"]
        return [k for k in rd if not (isinstance(k, tuple) and k[0] == "ps")], list(wr) + ps

    def _waits(self, e, rd, wr):
        rd, wr = self._excl(rd, wr)
        deps = set()
        for k in rd:
            if k in self.lastw:
                deps.add(self.lastw[k])
        for k in wr:
            if k in self.lastw:
                deps.add(self.lastw[k])
            deps |= self.readers.get(k, set())
        need = {}
        for (se, c) in deps:
            if se == e and e == "pe":
                continue
            if self.waited[e].get(se, 0) >= c:
                continue
            need[se] = max(need.get(se, 0), c)
        for se, c in need.items():
            self.eng[e].wait_ge(self._semof((se, c)), c)
            self.waited[e][se] = c

    def _reg(self, tok, rd, wr):
        rd, wr = self._excl(rd, wr)
        for k in wr:
            self.lastw[k] = tok
            self.readers[k] = set()
        for k in rd:
            self.readers.setdefault(k, set()).add(tok)

    def op(self, e, fn, rd=(), wr=()):
        self._waits(e, rd, wr)
        ins = fn(self.eng[e])
        self.cnt[e] += 1
        ins.then_inc(self.sem[e], 1)
        self._reg((e, self.cnt[e]), rd, wr)

    def pe(self, fns, rd=(), wr=()):
        self._waits("pe", rd, wr)
        ins = None
        for f in fns:
            ins = f(self.nc.tensor)
        self.cnt["pe"] += 1
        ins.then_inc(self.sem["pe"], 1)
        self._reg(("pe", self.cnt["pe"]), rd, wr)

    def dma(self, q, out, in_, rd=(), wr=(), key=None):
        assert key is not None
        dk = "d_" + key
        if dk not in self.dsem:
            self.dsem[dk] = self.nc.alloc_semaphore(dk)
            self.dcnt[dk] = 0
        self._waits(q, rd, wr)
        self.eng[q].dma_start(out=out, in_=in_).then_inc(self.dsem[dk], 16)
        self.dcnt[dk] += 16
        self._reg((dk, self.dcnt[dk]), rd, wr)

    def custom(self, e, fn, semname, inc, rd=(), wr=()):
        dk = "c_" + semname
        if dk not in self.dsem:
            self.dsem[dk] = self.nc.alloc_semaphore(dk)
            self.dcnt[dk] = 0
        self._waits(e, rd, wr)
        fn(self.eng[e]).then_inc(self.dsem[dk])
        self.dcnt[dk] += inc
        self._reg((dk, self.dcnt[dk]), rd, wr)

    def barrier(self, skip_prefix=None, keep=()):
        kept = {k: self.lastw[k] for k in keep if k in self.lastw}
        for e in self.eng:
            for se, c in self.cnt.items():
                if se != e and c > self.waited[e].get(se, 0):
                    self.eng[e].wait_ge(self.sem[se], c)
                    self.waited[e][se] = c
            for dk, c in self.dcnt.items():
                if skip_prefix is not None and dk.startswith(skip_prefix):
                    continue
                if c > self.waited[e].get(dk, 0):
                    self.eng[e].wait_ge(self.dsem[dk], c)
                    self.waited[e][dk] = c
        for e in ("act", "dve", "pool"):
            c = self.cnt[e]
            if c > self.waited[e].get(e, 0):
                self.eng[e].wait_ge(self.sem[e], c)
                self.waited[e][e] = c
        self.lastw = dict(kept)
        self.readers = {}


def build_nc():
    nc = bass.Bass("TRN2", target_bir_lowering=False)
    S = Sched(nc)

    def din(name, shape):
        return nc.dram_tensor(name, list(shape), F32, kind="ExternalInput").ap()

    x = din("x", [NT, D])
    cT = din("cT", [128, 8])
    gsel = din("gsel", [128, 3])
    w_ada = din("w_ada", [D, 9 * D])
    b_ada = din("b_ada", [128, 72])
    f1g = din("ffn1_w_gate", [D, DFF]); f1u = din("ffn1_w_up", [D, DFF]); f1d = din("ffn1_w_down", [DFF, D])
    f2g = din("ffn2_w_gate", [D, DFF]); f2u = din("ffn2_w_up", [D, DFF]); f2d = din("ffn2_w_down", [DFF, D])
    lnp = din("lnp", [128, 48])
    w_in = din("w_in", [D, 3592])
    fbf = din("fox_b_f", [8, 1])
    gnp = din("gnp", [128, 8])
    w_o = din("w_o", [D, D])
    rope_prev = din("rope_prev", [128, 16, 128])
    rope_own = din("rope_own", [128, 16, 128])
    decT_in = din("decT", [128, 4, 128])
    qkdec_in = din("qkdec", [128, 72])
    mask_in = din("maskT", [128, 128])
    ident_in = din("ident", [128, 128])
    out = nc.dram_tensor("out", [NT, D], F32, kind="ExternalOutput").ap()
    x1sp = nc.dram_tensor("x1sp", [128, 8, NT], F32).ap()
    qaug_d = nc.dram_tensor("qaug_d", [8, 6, NT], BF16).ap()
    kaug_d = nc.dram_tensor("kaug_d", [8, 3, 2 * NT], BF16).ap()
    cdram = nc.dram_tensor("cdram", [8, 2 * NT], F32).ap()
    oT_d = nc.dram_tensor("oT_d", [128, 8, NT], BF16).ap()
    ibs = [nc.dram_tensor(f"ib{q}", [512, 1024], BF16) for q in range(4)]
    obs = [nc.dram_tensor(f"ob{q}", [1024, 1024], BF16) for q in range(4)]
    dbg = None
    if DEBUG is not None:
        dbg = nc.dram_tensor("dbg", [128, 8, NT], F32, kind="ExternalOutput").ap()

    PS = nc.alloc_psum_tensor("ps", [128, 8, 512], F32).ap()

    def psb(b):
        return PS[:, b, :]

    with contextlib.ExitStack() as G:
        _ctr = [0]

        def sb(name, shape, dt, st=G):
            _ctr[0] += 1
            return st.enter_context(nc.sbuf_tensor(f"{name}_{_ctr[0]}", list(shape), dt)).ap()

        ident_f = sb("ident_f", [128, 128], F32)
        ident_b = sb("ident_b", [128, 128], BF16)
        ones_b = sb("ones_b", [128, 128], BF16)
        mask_f = sb("mask_f", [128, 128], F32)
        mask_b = sb("mask_b", [128, 128], BF16)
        gs = sb("gs", [128, 3], F32)
        modfm = sb("modfm", [128, 72], F32)
        badaf = sb("badaf", [128, 72], F32)
        lnv = sb("lnv", [128, 48], F32)
        vec = sb("vec", [128, 16, 8], F32)
        gnv = sb("gnv", [128, 8], F32)
        qkdec = sb("qkdec_s", [128, 72], F32)
        ctile = sb("ctile", [128, 8], F32)
        cact = sb("cact", [128, 8], BF16)
        fb = sb("fb", [8, 1], F32)
        hT = sb("hT", [128, 8, NT], BF16)

        SC1P, SH1, G1H, A2, B2, G2, A3, B3, G3H, AG1, AB1, AG2, AB2 = range(13)

        def V(i, dc):
            return vec[:, i, dc:dc + 1]

        S.dma("sp", ident_f, ident_in, wr=["ident_f"], key="c0_1")
        S.dma("sp", mask_f, mask_in, wr=["mask_f"], key="c0_2")
        S.dma("sp", gs, gsel, wr=["gs"], key="c0_3")
        S.dma("sp", badaf, b_ada, wr=["badaf"], key="c0_4")
        S.dma("sp", lnv, lnp, wr=["lnv"], key="c0_5")
        S.dma("sp", gnv, gnp, wr=["gnv"], key="c0_6")
        S.dma("sp", qkdec, qkdec_in, wr=["qkdec"], key="c0_7")
        S.dma("sp", ctile, cT, wr=["ctile"], key="c0_8")
        S.dma("sp", fb, fbf, wr=["fb"], key="c0_9")
        S.op("dve", lambda e: e.tensor_copy(out=ident_b, in_=ident_f), rd=["ident_f"], wr=["ident_b"])
        S.op("dve", lambda e: e.tensor_copy(out=mask_b, in_=mask_f), rd=["mask_f"], wr=["mask_b"])
        S.op("dve", lambda e: e.memset(ones_b, 1.0), wr=["ones_b"])
        epsc = sb("epsc", [128, 1], F32)
        S.op("dve", lambda e: e.memset(epsc, EPS), wr=["epsc"])
        S.op("act", lambda e: e.activation(out=cact, in_=ctile, func=AF.Silu), rd=["ctile"], wr=["cact"])

        wa_v = w_ada.rearrange("(dc p) f -> p dc f", p=128)

        with contextlib.ExitStack() as P1:
            xT = sb("xT", [128, 8, NT], F32, P1)

            def M(i):
                return modfm[:, i * 8:(i + 1) * 8]

            def L(i):
                return lnv[:, i * 8:(i + 1) * 8]

            def mod_dma(v, buf, bkey):
                S.dma("pool", buf, wa_v[:, :, v * 1024:(v + 1) * 1024], wr=[bkey], key=f"wabd_{bkey}")

            def mod_mm(v, buf, bkey):
                fns = []
                for k in range(8):
                    for dc in range(8):
                        fns.append(lambda e, k=k, dc=dc: e.matmul(
                            PS[:, 6, k:k + 1], lhsT=buf[:, dc, k * 128:(k + 1) * 128],
                            rhs=cact[:, dc:dc + 1], start=(dc == 0), stop=(dc == 7)))
                S.pe(fns, rd=[bkey, "cact"], wr=[("ps", 6)])
                S.op("dve", lambda e: e.tensor_tensor(out=M(v), in0=PS[:, 6, 0:8], in1=badaf[:, v * 8:(v + 1) * 8], op=ALU.add),
                     rd=[("ps", 6), "badaf"], wr=[("modfm", v)])

            def vop(fn, w, rm=(), rv=()):
                S.op("dve", fn, rd=[("modfm", m) for m in rm] + ["lnv"] + [("vec", r) for r in rv], wr=[("vec", w)])

            def vec_after(v):
                if v == 2:
                    vop(lambda e: e.tensor_scalar(out=vec[:, G1H, :], in0=M(2), scalar1=0.5, scalar2=None, op0=ALU.mult), G1H, rm=[2])
                if v == 4:
                    vop(lambda e: e.tensor_scalar(out=vec[:, 15, :], in0=M(4), scalar1=1.0, scalar2=None, op0=ALU.add), 15, rm=[4])
                    vop(lambda e: e.tensor_tensor(out=vec[:, A2, :], in0=L(0), in1=vec[:, 15, :], op=ALU.mult), A2, rv=[15])
                    vop(lambda e: e.tensor_tensor(out=vec[:, B2, :], in0=L(1), in1=vec[:, 15, :], op=ALU.mult), B2, rv=[15])
                    vop(lambda e: e.tensor_tensor(out=vec[:, B2, :], in0=vec[:, B2, :], in1=M(3), op=ALU.add), B2, rm=[3], rv=[B2])
                if v == 5:
                    vop(lambda e: e.tensor_copy(out=vec[:, G2, :], in_=M(5)), G2, rm=[5])
                if v == 7:
                    vop(lambda e: e.tensor_scalar(out=vec[:, 15, :], in0=M(7), scalar1=1.0, scalar2=None, op0=ALU.add), 15, rm=[7], rv=[15])
                    vop(lambda e: e.tensor_tensor(out=vec[:, A3, :], in0=L(2), in1=vec[:, 15, :], op=ALU.mult), A3, rv=[15])
                    vop(lambda e: e.tensor_tensor(out=vec[:, B3, :], in0=L(3), in1=vec[:, 15, :], op=ALU.mult), B3, rv=[15])
                    vop(lambda e: e.tensor_tensor(out=vec[:, B3, :], in0=vec[:, B3, :], in1=M(6), op=ALU.add), B3, rm=[6], rv=[B3])
                if v == 8:
                    vop(lambda e: e.tensor_scalar(out=vec[:, G3H, :], in0=M(8), scalar1=0.5, scalar2=None, op0=ALU.mult), G3H, rm=[8])

            wpc = [sb(f"wpc{i}", [128, 8, 256], BF16, P1) for i in range(2)]
            hstate = [0]

            def ffn1_hook():
                i = hstate[0]
                hstate[0] += 1
                if i < 28:
                    v, q = 2 + i // 4, i % 4
                    S.dma("pool", wpc[i % 2], wa_v[:, :, v * 1024 + q * 256: v * 1024 + (q + 1) * 256], wr=[("wpc", i % 2)],
                          key=f"wpc{i % 2}")
                n = i - 1
                if 0 <= n < 28:
                    v, q = 2 + n // 4, n % 4
                    buf = wpc[n % 2]
                    fns = []
                    for kk in range(2):
                        for dc in range(8):
                            fns.append(lambda e, kk=kk, dc=dc: e.matmul(
                                PS[:, 6, q * 2 + kk: q * 2 + kk + 1], lhsT=buf[:, dc, kk * 128:(kk + 1) * 128],
                                rhs=cact[:, dc:dc + 1], start=(dc == 0), stop=(dc == 7)))
                    S.pe(fns, rd=[("wpc", n % 2), "cact"], wr=[("ps", 6)])
                    if q == 3:
                        S.op("dve", lambda e: e.tensor_tensor(out=M(v), in0=PS[:, 6, 0:8], in1=badaf[:, v * 8:(v + 1) * 8], op=ALU.add),
                             rd=[("ps", 6), "badaf"], wr=[("modfm", v)])
                        vec_after(v)

            with contextlib.ExitStack() as P0:
                wab0 = sb("wab0", [128, 8, 1024], BF16, P0)
                wab1 = sb("wab1", [128, 8, 1024], BF16, P0)
                mod_dma(1, wab1, "wab1")
                mod_dma(0, wab0, "wab0")
                vop(lambda e: e.tensor_scalar(out=vec[:, AG1, :], in0=L(0), scalar1=ALPHA, scalar2=None, op0=ALU.mult), AG1)
                vop(lambda e: e.tensor_scalar(out=vec[:, AB1, :], in0=L(1), scalar1=ALPHA, scalar2=None, op0=ALU.mult), AB1)
                vop(lambda e: e.tensor_scalar(out=vec[:, AG2, :], in0=L(2), scalar1=ALPHA, scalar2=None, op0=ALU.mult), AG2)
                vop(lambda e: e.tensor_scalar(out=vec[:, AB2, :], in0=L(3), scalar1=ALPHA, scalar2=None, op0=ALU.mult), AB2)
                vop(lambda e: e.tensor_copy(out=vec[:, 13, :], in_=L(4)), 13)
                vop(lambda e: e.tensor_copy(out=vec[:, 14, :], in_=L(5)), 14)
                mod_mm(1, wab1, "wab1")
                mod_mm(0, wab0, "wab0")
                vop(lambda e: e.tensor_scalar(out=vec[:, SC1P, :], in0=M(1), scalar1=1.0, scalar2=None, op0=ALU.add), SC1P, rm=[1])
                vop(lambda e: e.tensor_copy(out=vec[:, SH1, :], in_=M(0)), SH1, rm=[0])

                xtok = [sb(f"xtok{i}", [128, D], F32, P0) for i in range(2)]
                for blk in range(16):
                    sl = blk % 2
                    S.dma("sp", xtok[sl], x[blk * 128:(blk + 1) * 128, :], wr=[("xtok", sl)], key=f"xtok{sl}")
                    b0 = 1 + 2 * sl
                    pv = PS[:, b0:b0 + 2, :].rearrange("p b (k c) -> p (b k) c", c=128)
                    S.pe([lambda e, dc=dc, sl=sl, pv=pv: e.transpose(out=pv[:, dc, :], in_=xtok[sl][:, dc * 128:(dc + 1) * 128],
                                                                    identity=ident_f) for dc in range(8)],
                         rd=[("xtok", sl), "ident_f"], wr=[("ps", b0), ("ps", b0 + 1)])
                    for hb in range(2):
                        S.op("dve", lambda e, pv=pv, blk=blk, hb=hb: e.tensor_scalar(
                            out=xT[:, hb * 4:hb * 4 + 4, blk * 128:(blk + 1) * 128], in0=pv[:, hb * 4:hb * 4 + 4, :],
                            scalar1=ALPHA, scalar2=None, op0=ALU.mult),
                            rd=[("ps", b0 + hb)], wr=[("xT", dc, blk // 4) for dc in range(hb * 4, hb * 4 + 4)])
                    for dc in range(8):
                        if dc < 4:
                            S.op("act", lambda e, pv=pv, dc=dc, blk=blk: e.activation(
                                out=hT[:, dc, blk * 128:(blk + 1) * 128], in_=pv[:, dc, :], func=AF.Identity,
                                scale=V(SC1P, dc), bias=V(SH1, dc)),
                                rd=[("ps", b0), ("vec", SC1P), ("vec", SH1)], wr=[("hT", dc, blk // 4)])
                        else:
                            S.op("dve", lambda e, pv=pv, dc=dc, blk=blk: e.tensor_scalar(
                                out=hT[:, dc, blk * 128:(blk + 1) * 128], in0=pv[:, dc, :],
                                scalar1=V(SC1P, dc), scalar2=V(SH1, dc), op0=ALU.mult, op1=ALU.add),
                                rd=[("ps", b0 + 1), ("vec", SC1P), ("vec", SH1)], wr=[("hT", dc, blk // 4)])

            if DEBUG == "p0b":
                for tt in range(4):
                    S.dma("sp", dbg[:, :, tt * 512:(tt + 1) * 512], xT[:, :, tt * 512:(tt + 1) * 512],
                          rd=[("xT", dc, tt) for dc in range(8)], wr=[("dbg", tt)], key=f"dbg{tt}")
                S.barrier()
                return nc

            def ffn(st, wg, wu, wd, GV, xTt, hook=None, ln_cb=None):
                aT = sb("aT", [128, 11, NT], BF16, st)
                wgu = [sb(f"wgu{i}", [128, 2, 8, 128], BF16, st) for i in range(2)]
                if ln_cb is None:
                    wdb = [sb(f"wdb{i}", [128, 11, 128], BF16, st) for i in range(2)]
                else:
                    wdall = sb("wdall", [128, 8, 11, 128], BF16, st)
                    wdb = [wdall[:, 0], wdall[:, 1]]
                nst = 2 if ln_cb is None else 1
                stmp = [sb(f"stmp{i}", [128, 512], F32, st) for i in range(nst)]
                wg_v = wg.rearrange("(dc p) f -> p dc f", p=128)
                wu_v = wu.rearrange("(dc p) f -> p dc f", p=128)
                wd_v = wd.rearrange("(fc p) d -> p fc d", p=128)
                ucnt = 0
                wcnt = 0
                dcnt = 0
                for half in range(2):
                    for fi in range(11):
                        fc = half * 11 + fi
                        if hook is not None:
                            hook()
                        sl = wcnt % 2
                        wcnt += 1
                        S.dma("pool", wgu[sl][:, 0], wg_v[:, :, fc * 128:(fc + 1) * 128], wr=[("wgu", sl)], key=f"wgu{sl}")
                        S.dma("pool", wgu[sl][:, 1], wu_v[:, :, fc * 128:(fc + 1) * 128], wr=[("wgu", sl)], key=f"wgu{sl}")
                        if ln_cb is not None and half == 1 and 1 <= fi <= 8:
                            dq = fi - 1
                            S.dma("pool", wdall[:, dq], wd_v[:, 11:22, dq * 128:(dq + 1) * 128],
                                  wr=[("wdall", dq)] + ([("wdb", dq)] if dq < 2 else []), key=f"wo{dq}")
                        for tt in range(4):
                            pb = (ucnt % 2) * 2
                            ts = slice(tt * 512, (tt + 1) * 512)
                            fns = []
                            for gu in range(2):
                                for dc in range(8):
                                    fns.append(lambda e, gu=gu, dc=dc, sl=sl, pb=pb, ts=ts: e.matmul(
                                        PS[:, pb + gu, :], lhsT=wgu[sl][:, gu, dc, :], rhs=hT[:, dc, ts],
                                        start=(dc == 0), stop=(dc == 7)))
                            S.pe(fns, rd=[("wgu", sl)] + [("hT", dc, tt) for dc in range(8)], wr=[("ps", pb), ("ps", pb + 1)])
                            st_i = ucnt % nst
                            S.op("act", lambda e, pb=pb, st_i=st_i: e.activation(out=stmp[st_i], in_=PS[:, pb, :], func=AF.Silu),
                                 rd=[("ps", pb)], wr=[("stmp", st_i)])
                            S.op("dve", lambda e, pb=pb, st_i=st_i, fi=fi, ts=ts: e.tensor_tensor(
                                out=aT[:, fi, ts], in0=stmp[st_i], in1=PS[:, pb + 1, :], op=ALU.mult),
                                rd=[("stmp", st_i), ("ps", pb + 1)], wr=[("aT", fi, tt)])
                            ucnt += 1
                    if ln_cb is not None and half == 1:
                        st_fn, ap_fn = ln_cb

                        def Dtile(tt):
                            nonlocal dcnt
                            ts = slice(tt * 512, (tt + 1) * 512)
                            for dch in range(8):
                                bank = 4 + dcnt % 2
                                dcnt += 1
                                S.pe([lambda e, fi=fi, dch=dch, bank=bank, ts=ts: e.matmul(
                                    PS[:, bank, :], lhsT=wdall[:, dch, fi, :], rhs=aT[:, fi, ts], start=(fi == 0), stop=(fi == 10))
                                    for fi in range(11)],
                                    rd=[("wdall", dch)] + [("aT", fi, tt) for fi in range(11)], wr=[("ps", bank)])
                                S.op("dve", lambda e, bank=bank, dch=dch, ts=ts: e.scalar_tensor_tensor(
                                    out=xTt[:, dch, ts], in0=PS[:, bank, :], scalar=V(GV, dch), in1=xTt[:, dch, ts],
                                    op0=ALU.mult, op1=ALU.add),
                                    rd=[("ps", bank), ("vec", GV), ("xT", dch, tt)], wr=[("xT", dch, tt)])
                        Dtile(0)
                        Dtile(1)
                        st_fn(0)
                        Dtile(2)
                        st_fn(1)
                        ap_fn(0)
                        Dtile(3)
                        st_fn(2)
                        ap_fn(1)
                        st_fn(3)
                        ap_fn(2)
                        ap_fn(3)
                        continue
                    for dch in range(8):
                        if hook is not None:
                            hook()
                        sl = dcnt % 2
                        S.dma("pool", wdb[sl], wd_v[:, half * 11:(half + 1) * 11, dch * 128:(dch + 1) * 128],
                              wr=[("wdb", sl)], key=f"wdb{sl}")
                        for tt in range(4):
                            bank = 4 + (dcnt * 4 + tt) % 2
                            ts = slice(tt * 512, (tt + 1) * 512)
                            S.pe([lambda e, fi=fi, sl=sl, bank=bank, ts=ts: e.matmul(
                                PS[:, bank, :], lhsT=wdb[sl][:, fi, :], rhs=aT[:, fi, ts], start=(fi == 0), stop=(fi == 10))
                                for fi in range(11)],
                                rd=[("wdb", sl)] + [("aT", fi, tt) for fi in range(11)], wr=[("ps", bank)])
                            S.op("dve", lambda e, bank=bank, dch=dch, ts=ts: e.scalar_tensor_tensor(
                                out=xTt[:, dch, ts], in0=PS[:, bank, :], scalar=V(GV, dch), in1=xTt[:, dch, ts],
                                op0=ALU.mult, op1=ALU.add),
                                rd=[("ps", bank), ("vec", GV), ("xT", dch, tt)], wr=[("xT", dch, tt)])
                        dcnt += 1

            def ln_stats(xTt, tt, lnb):
                yb, ys, mean, msq, rstd, nmr, tmpv = lnb
                p = tt % 2
                ts = slice(tt * 512, (tt + 1) * 512)
                for dc in range(8):
                    s4 = dc % len(yb)
                    S.op("act", lambda e, dc=dc, s4=s4: e.activation(out=yb[s4], in_=xTt[:, dc, ts], func=AF.Copy),
                         rd=[("xT", dc, tt)], wr=[("yb", s4)])
                    S.op("dve", lambda e, dc=dc, s4=s4: e.tensor_tensor(out=ys[s4], in0=xTt[:, dc, ts], in1=xTt[:, dc, ts], op=ALU.mult),
                         rd=[("xT", dc, tt)], wr=[("ys", s4)])
                    S.pe([lambda e, dc=dc, s4=s4: e.matmul(PS[:, 6, :], lhsT=ones_b, rhs=yb[s4], start=(dc == 0), stop=(dc == 7)),
                          lambda e, dc=dc, s4=s4: e.matmul(PS[:, 7, :], lhsT=ones_b, rhs=ys[s4], start=(dc == 0), stop=(dc == 7))],
                         rd=["ones_b", ("yb", s4), ("ys", s4)], wr=[("ps", 6), ("ps", 7)])
                S.op("dve", lambda e: e.tensor_scalar(out=mean[p], in0=PS[:, 6, :], scalar1=1.0 / D, scalar2=None, op0=ALU.mult),
                     rd=[("ps", 6)], wr=[("mean", p)])
                S.op("dve", lambda e: e.tensor_tensor(out=msq[p], in0=mean[p], in1=mean[p], op=ALU.mult),
                     rd=[("mean", p)], wr=[("msq", p)])
                S.op("dve", lambda e: e.scalar_tensor_tensor(out=msq[p], in0=PS[:, 7, :], scalar=1.0 / D, in1=msq[p],
                                                             op0=ALU.mult, op1=ALU.subtract),
                     rd=[("ps", 7), ("msq", p)], wr=[("msq", p)])
                S.op("act", lambda e: e.activation(out=rstd[p], in_=msq[p], func=AF.Sqrt, bias=epsc[:, 0:1], scale=1.0),
                     rd=[("msq", p), "epsc"], wr=[("rstd", p)])
                S.op("dve", lambda e: e.reciprocal(out=rstd[p], in_=rstd[p]), rd=[("rstd", p)], wr=[("rstd", p)])
                S.op("dve", lambda e: e.scalar_tensor_tensor(out=nmr[p], in0=mean[p], scalar=-1.0, in1=rstd[p],
                                                             op0=ALU.mult, op1=ALU.mult),
                     rd=[("mean", p), ("rstd", p)], wr=[("nmr", p)])

            def ln_apply(xTt, tt, GA, GB, HA, HB, lnb):
                yb, ys, mean, msq, rstd, nmr, tmpv = lnb
                p = tt % 2
                ts = slice(tt * 512, (tt + 1) * 512)
                for dc in range(8):
                    tv = tmpv[dc % 2]
                    S.op("dve", lambda e, dc=dc, tv=tv: e.tensor_tensor(out=tv, in0=xTt[:, dc, ts], in1=rstd[p], op=ALU.mult),
                         rd=[("xT", dc, tt), ("rstd", p)], wr=[("tmpv", dc % 2)])
                    S.op("dve", lambda e, tv=tv: e.tensor_tensor(out=tv, in0=tv, in1=nmr[p], op=ALU.add),
                         rd=[("tmpv", dc % 2), ("nmr", p)], wr=[("tmpv", dc % 2)])
                    S.op("act", lambda e, dc=dc, tv=tv: e.activation(out=xTt[:, dc, ts], in_=tv, func=AF.Identity,
                                                                     scale=V(GA, dc), bias=V(GB, dc)),
                         rd=[("tmpv", dc % 2), ("vec", GA), ("vec", GB)], wr=[("xT", dc, tt)])
                    if HA is not None:
                        S.op("act", lambda e, dc=dc, tv=tv: e.activation(out=hT[:, dc, ts], in_=tv, func=AF.Identity,
                                                                         scale=V(HA, dc), bias=V(HB, dc)),
                             rd=[("tmpv", dc % 2), ("vec", HA), ("vec", HB)], wr=[("hT", dc, tt)])

            def ln_all(xTt, GA, GB, HA, HB, lnb, before=None, after=None):
                if before:
                    before(0)
                ln_stats(xTt, 0, lnb)
                for tt in range(4):
                    if tt + 1 < 4:
                        if before:
                            before(tt + 1)
                        ln_stats(xTt, tt + 1, lnb)
                    ln_apply(xTt, tt, GA, GB, HA, HB, lnb)
                    if after:
                        after(tt)

            def ln_bufs(st, nrot=4):
                def two(nm):
                    return [sb(f"{nm}{i}", [128, 512], F32, st) for i in range(2)]
                return ([sb(f"yb{i}", [128, 512], BF16, st) for i in range(nrot)],
                        [sb(f"ys{i}", [128, 512], BF16, st) for i in range(nrot)],
                        two("mean"), two("msq"), two("rstd"), two("nmr"), two("tmpv"))

            with contextlib.ExitStack() as F1:
                ffn(F1, f1g, f1u, f1d, G1H, xT, hook=ffn1_hook)
                if DEBUG == "f1":
                    S.dma("sp", dbg, xT, rd=[("xT", dc, tt) for dc in range(8) for tt in range(4)], wr=["dbg"], key="dbg")
                    S.barrier()
                    return nc
                lnb = ln_bufs(F1)
                def after1(tt):
                    ts = slice(tt * 512, (tt + 1) * 512)
                    ibv = ibs[tt].ap().rearrange("a (two t) -> (a two) t", two=2).rearrange("(dc p) t -> p dc t", p=128)
                    S.dma("sp", ibv, hT[:, :, ts], rd=[("hT", dc, tt) for dc in range(8)], wr=[("ib", tt)], key=f"ib{tt}")
                    S.custom("pool", lambda e, tt=tt: e.collective_compute(
                        "AllGather", ALU.bypass, replica_groups=[[0, 1], [2, 3], [4, 5], [6, 7]],
                        ins=[ibs[tt].ap().opt()], outs=[obs[tt].ap().opt()]), f"cc{tt}", 1,
                        rd=[("ib", tt)], wr=[("ob", tt)])
                    S.dma("sp", x1sp[:, :, ts], xT[:, :, ts], rd=[("xT", dc, tt) for dc in range(8)], wr=[("x1sp", tt)],
                          key=f"x1sp{tt}")
                ln_all(xT, AG1, AB1, A2, B2, lnb, after=after1)
            if DEBUG == "x1":
                S.dma("sp", dbg, xT, rd=[("xT", dc, tt) for dc in range(8) for tt in range(4)], wr=["dbg"], key="dbg")
        if DEBUG == "x1":
            S.barrier()
            return nc
        S.barrier(skip_prefix="c_cc", keep=[("ob", tt) for tt in range(4)])

        win_v = w_in.rearrange("(dc p) f -> p dc f", p=128)
        with contextlib.ExitStack() as P2:
            hprev = sb("hprev", [128, 8, NT], BF16, P2)
            for q in range(4):
                obv = obs[q].ap()[0:512, :].rearrange("a (two t) -> (a two) t", two=2).rearrange("(dc p) t -> p dc t", p=128)
                S.dma("sp", hprev[:, :, q * 512:(q + 1) * 512], obv, rd=[("ob", q)], wr=[("hprev", q)], key=f"hprev{q}")

            def hsrc(kt):
                return (hprev, "hprev") if kt < 4 else (hT, "hT")

            wrk_pre = sb("wrk_pre", [128, 8, 512], BF16, P2)
            wrv_pre = sb("wrv_pre", [128, 8, 512], BF16, P2)
            PAB = contextlib.ExitStack()
            Vaug = sb("Vaug", [128, 32, 8, 65], BF16, PAB)
            KTA0 = sb("KTA0", [128, 2 * NT], BF16, PAB)
            KTB0 = sb("KTB0", [128, 2 * NT], BF16, PAB)
            QTA0 = sb("QTA0", [128, NT], BF16, PAB)
            QTB0 = sb("QTB0", [128, NT], BF16, PAB)
            wqkv0 = sb("wqkv0", [128, 8, 256], BF16, PAB)
            p0bank = [0]

            def p0_q(qg):
                bank = 4 + p0bank[0] % 2
                p0bank[0] += 1
                ts = slice(qg * 512, (qg + 1) * 512)
                S.pe([lambda e, dc=dc: e.matmul(PS[:, bank, :], lhsT=wqkv0[:, dc, 0:128], rhs=hT[:, dc, ts],
                                                start=(dc == 0), stop=(dc == 7)) for dc in range(8)],
                     rd=["wqkv0"], wr=[("ps", bank)])
                S.op("dve", lambda e: e.tensor_scalar(out=QTA0[0:64, ts], in0=PS[0:64, bank, :], scalar1=0.125,
                                                      scalar2=None, op0=ALU.mult), rd=[("ps", bank)], wr=[("QTA0", qg)])
                S.op("dve", lambda e: e.tensor_scalar(out=QTB0[64:128, ts], in0=PS[64:128, bank, :], scalar1=0.125,
                                                      scalar2=None, op0=ALU.mult), rd=[("ps", bank)], wr=[("QTB0", qg)])

            def p0_k(kt):
                src, sk = hsrc(kt)
                bank = 4 + p0bank[0] % 2
                p0bank[0] += 1
                tsl = slice((kt % 4) * 512, (kt % 4 + 1) * 512)
                S.pe([lambda e, dc=dc: e.matmul(PS[:, bank, :], lhsT=wqkv0[:, dc, 128:256], rhs=src[:, dc, tsl],
                                                start=(dc == 0), stop=(dc == 7)) for dc in range(8)],
                     rd=["wqkv0"] + ([("hprev", kt)] if kt < 4 else []), wr=[("ps", bank)])
                S.op("act", lambda e: e.activation(out=KTA0[0:64, kt * 512:(kt + 1) * 512], in_=PS[0:64, bank, :], func=AF.Copy),
                     rd=[("ps", bank)], wr=[("KTA0", kt)])
                S.op("act", lambda e: e.activation(out=KTB0[64:128, kt * 512:(kt + 1) * 512], in_=PS[64:128, bank, :], func=AF.Copy),
                     rd=[("ps", bank)], wr=[("KTB0", kt)])
            with contextlib.ExitStack() as PA:
                wfl = sb("wfl", [128, 8, 8], BF16, PA)
                wv = sb("wv", [128, 8, 512], BF16, PA)
                S.dma("pool", wfl, win_v[:, :, 1536:1544], wr=["wfl"], key="wfl")
                S.dma("pool", wv, win_v[:, :, 1024:1536], wr=["wv"], key="wv")
                S.op("dve", lambda e: e.memset(Vaug[:, :, :, 64:65], 1.0), wr=["Vones"])
                S.dma("pool", wqkv0[:, :, 0:128], win_v[:, :, 0:128], wr=["wqkv0"], key="wqkv0")
                S.dma("pool", wqkv0[:, :, 128:256], win_v[:, :, 512:640], wr=["wqkv0"], key="wqkv0")
                S.op("dve", lambda e: e.memset(KTA0[64:67, :], 1.0), wr=["KTA0aug"])
                S.op("dve", lambda e: e.memset(KTB0[0:64, :], 1.0), wr=["KTB0aug"])
                S.op("dve", lambda e: e.memset(QTB0[0:64, :], 0.0), wr=["QTB0aug"])
                La = sb("La", [8, 2 * NT], F32, PA)
                Lb = sb("Lb", [8, 2 * NT], F32, PA)
                nfb = sb("nfb", [8, 1], F32, PA)
                offs = sb("offs", [8, 1], F32, PA)
                S.op("dve", lambda e: e.tensor_scalar(out=nfb, in0=fb, scalar1=-1.0, scalar2=None, op0=ALU.mult),
                     rd=["fb"], wr=["nfb"])
                for kt in range(4, 8):
                    src, sk = hsrc(kt)
                    tsl = slice((kt % 4) * 512, (kt % 4 + 1) * 512)
                    bank = kt % 2
                    S.pe([lambda e, dc=dc, src=src, tsl=tsl, bank=bank: e.matmul(
                        PS[0:8, bank, :], lhsT=wfl[:, dc, :], rhs=src[:, dc, tsl], start=(dc == 0), stop=(dc == 7))
                        for dc in range(8)],
                        rd=["wfl"] + ([(sk, kt % 4)] if sk == "hprev" else [("hT", dc, kt % 4) for dc in range(8)]),
                        wr=[("ps", bank)])
                    S.op("act", lambda e, kt=kt, bank=bank: e.activation(
                        out=La[:, kt * 512:(kt + 1) * 512], in_=PS[0:8, bank, :], func=AF.Exp, scale=-1.0, bias=nfb),
                        rd=[("ps", bank), "nfb"], wr=[("La", kt)])
                for kb in range(16, 32):
                    src, sk = hsrc(kb // 4)
                    t0 = (kb % 16) * 128
                    bank = 2 + kb % 2
                    S.pe([lambda e, dc=dc, src=src, t0=t0, bank=bank: e.matmul(
                        PS[:, bank, :], lhsT=src[:, dc, t0:t0 + 128], rhs=wv[:, dc, :], start=(dc == 0), stop=(dc == 7))
                        for dc in range(8)],
                        rd=["wv"] + ([("hprev", kb // 4)] if kb < 16 else []), wr=[("ps", bank)])
                    S.op("act", lambda e, kb=kb, bank=bank: e.activation(
                        out=Vaug[:, kb, :, 0:64], in_=PS[:, bank, :].rearrange("p (h d) -> p h d", d=64), func=AF.Copy),
                        rd=[("ps", bank)], wr=[("V", kb)])
                for qg in range(4):
                    p0_q(qg)
                for kt in range(4, 8):
                    p0_k(kt)
                for kt in range(0, 4):
                    src, sk = hsrc(kt)
                    tsl = slice((kt % 4) * 512, (kt % 4 + 1) * 512)
                    bank = kt % 2
                    S.pe([lambda e, dc=dc, src=src, tsl=tsl, bank=bank: e.matmul(
                        PS[0:8, bank, :], lhsT=wfl[:, dc, :], rhs=src[:, dc, tsl], start=(dc == 0), stop=(dc == 7))
                        for dc in range(8)],
                        rd=["wfl"] + ([(sk, kt % 4)] if sk == "hprev" else [("hT", dc, kt % 4) for dc in range(8)]),
                        wr=[("ps", bank)])
                    S.op("act", lambda e, kt=kt, bank=bank: e.activation(
                        out=La[:, kt * 512:(kt + 1) * 512], in_=PS[0:8, bank, :], func=AF.Exp, scale=-1.0, bias=nfb),
                        rd=[("ps", bank), "nfb"], wr=[("La", kt)])
                S.op("act", lambda e: e.activation(out=Lb, in_=La, func=AF.Ln, scale=1.0, bias=1.0),
                     rd=[("La", kt) for kt in range(8)], wr=["Lb"])
                for kb in range(0, 16):
                    src, sk = hsrc(kb // 4)
                    t0 = (kb % 16) * 128
                    bank = 2 + kb % 2
                    S.pe([lambda e, dc=dc, src=src, t0=t0, bank=bank: e.matmul(
                        PS[:, bank, :], lhsT=src[:, dc, t0:t0 + 128], rhs=wv[:, dc, :], start=(dc == 0), stop=(dc == 7))
                        for dc in range(8)],
                        rd=["wv"] + ([("hprev", kb // 4)] if kb < 16 else []), wr=[("ps", bank)])
                    S.op("act", lambda e, kb=kb, bank=bank: e.activation(
                        out=Vaug[:, kb, :, 0:64], in_=PS[:, bank, :].rearrange("p (h d) -> p h d", d=64), func=AF.Copy),
                        rd=[("ps", bank)], wr=[("V", kb)])
                for kt in range(0, 4):
                    p0_k(kt)
                ones8t = sb("ones8t", [8, NT], F32, PA)
                S.op("dve", lambda e: e.memset(ones8t, 1.0), wr=["ones8"])
                S.op("dve", lambda e: e.tensor_tensor_scan(out=La[:, 0:NT], data0=ones8t, data1=Lb[:, 0:NT], initial=0.0,
                                                           op0=ALU.mult, op1=ALU.add),
                     rd=["Lb", "ones8"] + [("La", kt) for kt in range(8)], wr=["Cp"])
                S.op("dve", lambda e: e.tensor_tensor(out=offs, in0=La[:, NT - 1:NT], in1=gs[0:8, 0:1], op=ALU.mult),
                     rd=["Cp", "gs"], wr=["offs"])
                S.op("dve", lambda e: e.tensor_tensor_scan(out=La[:, NT:2 * NT], data0=ones8t, data1=Lb[:, NT:2 * NT],
                                                           initial=offs, op0=ALU.mult, op1=ALU.add),
                     rd=["Lb", "ones8", "offs", "Cp"], wr=["Co"])
                O3w = sb("O3w", [128, 3, 256], BF16, PA)
                S.op("dve", lambda e: e.memset(O3w, 1.0), wr=["O3w"])
                for hh in range(8):
                    S.dma("sp", qaug_d[hh, 3:6, :].rearrange("z (b f) -> b z f", f=256), O3w[0:8, :, :],
                          rd=["O3w"], wr=[("qaug1", hh)], key=f"kaug0_{hh}")
                C128 = sb("C128", [128, 256], F32, PA)
                r1w = sb("r1w", [128, 256], F32, PA)
                hfw = sb("hfw", [128, 256], F32, PA)
                H3q = sb("H3q", [128, 3, 256], BF16, PA)
                H3k = sb("H3k", [128, 3, 256], BF16, PA)
                S.dma("sp", cdram, La, rd=["Cp", "Co"], wr=["cdram"], key="cdram")
                S.dma("sp", C128, cdram.rearrange("h (b f) -> (h b) f", f=256), rd=["cdram"], wr=["C128"], key="c128")

                def split3w(first_op, H3x, hk):
                    S.op("dve", first_op, rd=["C128", "gs"], wr=["r1w"])
                    for z in range(3):
                        S.op("dve", lambda e, z=z: e.tensor_copy(out=H3x[:, z, :], in_=r1w), rd=["r1w"], wr=[(hk, z)])
                        if z < 2:
                            S.op("dve", lambda e, z=z: e.tensor_copy(out=hfw, in_=H3x[:, z, :]), rd=[(hk, z)], wr=["hfw"])
                            S.op("dve", lambda e: e.tensor_tensor(out=r1w, in0=r1w, in1=hfw, op=ALU.subtract),
                                 rd=["r1w", "hfw"], wr=["r1w"])

                split3w(lambda e: e.tensor_scalar(out=r1w, in0=C128, scalar1=-1.0, scalar2=None, op0=ALU.mult), H3q, "H3q")
                split3w(lambda e: e.tensor_scalar(out=r1w, in0=C128, scalar1=gs[:, 2:3], scalar2=None, op0=ALU.add), H3k, "H3k")
                for hh in range(8):
                    S.dma("sp", qaug_d[hh, 0:3, :].rearrange("z (b f) -> b z f", f=256), H3q[hh * 16 + 8:hh * 16 + 16, :, :],
                          rd=[("H3q", z) for z in range(3)], wr=[("qaug0", hh)], key=f"qaug0_{hh}")
                    S.dma("sp", kaug_d[hh, :, :].rearrange("z (b f) -> b z f", f=256), H3k[hh * 16:hh * 16 + 16, :, :],
                          rd=[("H3k", z) for z in range(3)], wr=[("kaug0", hh)], key=f"kaug0_{hh}")
            S.barrier()

            with contextlib.ExitStack() as PB:
                KTA = [KTA0, sb("KTA1", [128, 2 * NT], BF16, PB)]
                KTB = [KTB0, sb("KTB1", [128, 2 * NT], BF16, PB)]
                QTA = [QTA0, sb("QTA1", [128, NT], BF16, PB)]
                QTB = [QTB0, sb("QTB1", [128, NT], BF16, PB)]
                wqkv = [wqkv0, sb("wqkv1", [128, 8, 256], BF16, PB)]
                PT = [sb(f"PT{i}", [128, 2, 512], BF16, PB) for i in range(3)]
                otm = [sb("otm0", [128, 16, 128], BF16, PB)] * 2
                rden = [sb(f"rden{i}", [128, 4], F32, PB) for i in range(2)]
                oTs = [sb(f"oTs{i}", [128, 512], F32, PB) for i in range(2)]
                ost = [sb(f"ost{i}", [128, 4, 128], BF16, PB) for i in range(2)]
                for i in range(1, 2):
                    S.op("dve", lambda e, i=i: e.memset(KTA[i][64:67, :], 1.0), wr=[("KTAaug", i)])
                    S.op("dve", lambda e, i=i: e.memset(KTB[i][0:64, :], 1.0), wr=[("KTBaug", i)])
                    S.op("dve", lambda e, i=i: e.memset(QTB[i][0:64, :], 0.0), wr=[("QTBaug", i)])
                tbank = [0]

                def proj_tasks(hp):
                    s = hp % 2
                    tasks = []

                    def t_load():
                        S.dma("pool", wqkv[s][:, :, 0:128], win_v[:, :, hp * 128:(hp + 1) * 128], wr=[("wqkv", s)], key=f"wqkv{s}")
                        S.dma("pool", wqkv[s][:, :, 128:256], win_v[:, :, 512 + hp * 128:512 + (hp + 1) * 128], wr=[("wqkv", s)],
                              key=f"wqkv{s}")
                        S.dma("sp", QTA[s][64:70, :], qaug_d[2 * hp], wr=[("QTAaug", s)], key=f"qtaA{s}")
                        S.dma("sp", QTB[s][58:64, :], qaug_d[2 * hp + 1], wr=[("QTBaug", s)], key=f"qtaB{s}")
                        S.dma("sp", KTA[s][67:70, :], kaug_d[2 * hp], wr=[("KTAaug", s)], key=f"ktaA{s}")
                        S.dma("sp", KTB[s][61:64, :], kaug_d[2 * hp + 1], wr=[("KTBaug", s)], key=f"ktaB{s}")
                    tasks.append(t_load)
                    for qg in range(4):
                        def t_q(qg=qg):
                            bank = tbank[0] % 2
                            tbank[0] += 1
                            ts = slice(qg * 512, (qg + 1) * 512)
                            S.pe([lambda e, dc=dc: e.matmul(PS[:, bank, :], lhsT=wqkv[s][:, dc, 0:128], rhs=hT[:, dc, ts],
                                                            start=(dc == 0), stop=(dc == 7)) for dc in range(8)],
                                 rd=[("wqkv", s)], wr=[("ps", bank)])
                            S.op("dve", lambda e: e.tensor_scalar(out=QTA[s][0:64, ts], in0=PS[0:64, bank, :], scalar1=0.125,
                                                                  scalar2=None, op0=ALU.mult),
                                 rd=[("ps", bank)], wr=[("QTA", s, qg)])
                            S.op("dve", lambda e: e.tensor_scalar(out=QTB[s][64:128, ts], in0=PS[64:128, bank, :], scalar1=0.125,
                                                                  scalar2=None, op0=ALU.mult),
                                 rd=[("ps", bank)], wr=[("QTB", s, qg)])
                        tasks.append(t_q)
                    for kt in range(8):
                        def t_k(kt=kt):
                            src, sk = hsrc(kt)
                            bank = tbank[0] % 2
                            tbank[0] += 1
                            tsl = slice((kt % 4) * 512, (kt % 4 + 1) * 512)
                            S.pe([lambda e, dc=dc: e.matmul(PS[:, bank, :], lhsT=wqkv[s][:, dc, 128:256], rhs=src[:, dc, tsl],
                                                            start=(dc == 0), stop=(dc == 7)) for dc in range(8)],
                                 rd=[("wqkv", s)], wr=[("ps", bank)])
                            S.op("dve", lambda e: e.tensor_copy(out=KTA[s][0:64, kt * 512:(kt + 1) * 512], in_=PS[0:64, bank, :]),
                                 rd=[("ps", bank)], wr=[("KTA", s, kt)])
                            S.op("dve", lambda e: e.tensor_copy(out=KTB[s][64:128, kt * 512:(kt + 1) * 512], in_=PS[64:128, bank, :]),
                                 rd=[("ps", bank)], wr=[("KTB", s, kt)])
                        tasks.append(t_k)
                    return tasks

                S.dma("pool", wrk_pre, win_v[:, :, 2056:2568], wr=[("wr", 1)], key="wr1")
                S.dma("pool", wrv_pre, win_v[:, :, 2568:3080], wr=[("wr", 2)], key="wr2")
                S.dma("sp", QTA[0][64:70, :], qaug_d[0], wr=[("QTAaug", 0)], key="qtaA0")
                S.dma("sp", QTB[0][58:64, :], qaug_d[1], wr=[("QTBaug", 0)], key="qtaB0")
                S.dma("sp", KTA[0][67:70, :], kaug_d[0], wr=[("KTAaug", 0)], key="ktaA0")
                S.dma("sp", KTB[0][61:64, :], kaug_d[1], wr=[("KTBaug", 0)], key="ktaB0")
                gstep = 0
                qcnt = 0
                for hp in range(4):
                    s = hp % 2
                    pending = proj_tasks(hp + 1) if hp + 1 < 4 else []
                    ucount = 0
                    for hl in range(2):
                        h = 2 * hp + hl
                        KTx, QTx = (KTA[s], QTA[s]) if hl == 0 else (KTB[s], QTB[s])
                        r0, r1 = (0, 70) if hl == 0 else (0, 128)
                        kname, qname = ("KTA", "QTA") if hl == 0 else ("KTB", "QTB")
                        units = []
                        for qg in range(4):
                            nfull = 16 + 4 * qg
                            for kb in range(0, nfull, 2):
                                units.append((qg, kb, 2, 0))
                            for m in range(4):
                                units.append((qg, nfull + m, 1, m * 128))
                        deferred = []

                        def emit_S(i, KTx=KTx, QTx=QTx, r0=r0, r1=r1, kname=kname, qname=qname, units=units):
                            qg, kb, nb, c0 = units[i]
                            b0 = 2 + 2 * ((gstep + i) % 2)
                            fns = []
                            for z in range(nb):
                                fns.append(lambda e, z=z: e.matmul(
                                    PS[:, b0 + z, c0:512], lhsT=KTx[r0:r1, (kb + z) * 128:(kb + z + 1) * 128],
                                    rhs=QTx[r0:r1, qg * 512 + c0:(qg + 1) * 512], start=True, stop=(nb == 2)))
                            if nb == 1:
                                fns.append(lambda e: e.matmul(PS[:, b0, c0:c0 + 128], lhsT=ident_b, rhs=mask_b,
                                                              start=False, stop=True))
                            S.pe(fns, rd=[(kname, s, kb // 4), (kname + "aug", s), (qname, s, qg), (qname + "aug", s), "ident_b", "mask_b"],
                                 wr=[("ps", b0), ("ps", b0 + 1)] if nb == 2 else [("ps", b0)])

                        def emit_E(i, units=units):
                            qg, kb, nb, c0 = units[i]
                            b0 = 2 + 2 * ((gstep + i) % 2)
                            pi = (gstep + i) % 3
                            if nb == 2:
                                S.op("act", lambda e: e.activation(out=PT[pi], in_=PS[:, b0:b0 + 2, :], func=AF.Exp),
                                     rd=[("ps", b0), ("ps", b0 + 1)], wr=[("PT", pi)])
                            else:
                                S.op("act", lambda e: e.activation(out=PT[pi][:, 0, c0:512], in_=PS[:, b0, c0:512], func=AF.Exp),
                                     rd=[("ps", b0)], wr=[("PT", pi)])

                        def emit_PV(i, h=h, hl=hl, units=units, deferred=deferred):
                            nonlocal qcnt
                            qg, kb, nb, c0 = units[i]
                            nk = 16 + 4 * qg + 4
                            pi = (gstep + i) % 3
                            obk = 6 + qg % 2
                            S.pe([lambda e, z=z: e.matmul(PS[0:65, obk, c0:512], lhsT=Vaug[:, kb + z, h, :], rhs=PT[pi][:, z, c0:512],
                                                          start=(kb + z == 0), stop=(kb + z == nk - 1)) for z in range(nb)],
                                 rd=[("PT", pi)], wr=[("ps", obk)])
                            if kb + nb == nk:
                                q2 = qcnt % 2
                                qcnt += 1
                                S.op("dve", lambda e: e.tensor_copy(out=oTs[q2][0:65, :], in_=PS[0:65, obk, :]),
                                     rd=[("ps", obk)], wr=[("oTs", q2)])

                                def fin(q2=q2, qg=qg, hl=hl):
                                    tb = tbank[0] % 2
                                    tbank[0] += 1
                                    tv = PS[:, tb, 0:260].rearrange("p (n d) -> p n d", d=65)
                                    S.pe([lambda e, n=n: e.transpose(out=tv[:, n, :], in_=oTs[q2][0:65, n * 128:(n + 1) * 128],
                                                                     identity=ident_f[0:65, 0:65]) for n in range(4)],
                                         rd=[("oTs", q2), "ident_f"], wr=[("ps", tb)])
                                    S.op("dve", lambda e: e.reciprocal(out=rden[q2], in_=tv[:, :, 64]), rd=[("ps", tb)],
                                         wr=[("rden", q2)])
                                    S.op("dve", lambda e: e.tensor_tensor(
                                        out=otm[s][:, qg * 4:(qg + 1) * 4, hl * 64:(hl + 1) * 64], in0=tv[:, :, 0:64],
                                        in1=rden[q2].unsqueeze(2).broadcast_to([128, 4, 64]), op=ALU.mult),
                                        rd=[("ps", tb), ("rden", q2)], wr=[("otm", 0, qg, hl)])
                                deferred.append([i + 3, fin])

                        n_steps = len(units)
                        emit_S(0)
                        for i in range(n_steps):
                            for dfr in list(deferred):
                                if dfr[0] <= i:
                                    dfr[1]()
                                    deferred.remove(dfr)
                            if pending and ucount % 11 == 3:
                                pending.pop(0)()
                            ucount += 1
                            if i + 1 < n_steps:
                                emit_S(i + 1)
                            emit_E(i)
                            emit_PV(i)
                        for dfr in list(deferred):
                            dfr[1]()
                        gstep += n_steps
                    while pending:
                        pending.pop(0)()
                    for b4 in range(4):
                        tb = tbank[0] % 2
                        tbank[0] += 1
                        pvb = PS[:, tb, :].bitcast(BF16)[:, 0:512].rearrange("p (c t) -> p c t", t=128)
                        S.pe([lambda e, c=c, b4=b4, pvb=pvb: e.transpose(out=pvb[:, c, :], in_=otm[s][:, b4 * 4 + c, :],
                                                                        identity=ident_b) for c in range(4)],
                             rd=[("otm", 0, b4, hl) for hl in range(2)] + ["ident_b"], wr=[("ps", tb)])
                        oq = (hp * 4 + b4) % 2
                        S.op("dve", lambda e, oq=oq, pvb=pvb: e.tensor_copy(out=ost[oq], in_=pvb),
                             rd=[("ps", tb)], wr=[("ost", oq)])
                        S.dma("sp", oT_d[:, hp, b4 * 512:(b4 + 1) * 512].rearrange("p (c t) -> p c t", t=128), ost[oq],
                              rd=[("ost", oq)], wr=[("oTd", hp, b4)], key=f"qaug0_{oq}")
            S.barrier()
            PAB.close()

            with contextlib.ExitStack() as PC:
                wr_ = [sb("wr0", [128, 8, 512], BF16, PC), wrk_pre, wrv_pre, sb("wr3", [128, 8, 512], BF16, PC)]
                ropeP = sb("ropeP", [128, 16, 128], F32, PC)
                ropeO = sb("ropeO", [128, 16, 128], F32, PC)
                decT = sb("decT_s", [128, 4, 128], F32, PC)
                Ctab = sb("Ctab", [128, 4, 128], F32, PC)
                dmy = sb("dmy", [128, 4], F32, PC)
                S.op("dve", lambda e: e.memset(dmy, 1.0), wr=["dmy0", "dmy1"])
                St = sb("St", [128, 4, 128], F32, PC)
                Stmp = sb("Stmp", [128, 4, 128], F32, PC)

                def two(name, shape, dt, n=2):
                    return [sb(f"{name}{i}", shape, dt, PC) for i in range(n)]

                Sbf = two("Sbf", [128, 4, 128], BF16, 3)
                qkrb = two("qkrb", [128, 2, 4, 128], BF16)
                qrb = [qkrb[0][:, 0], qkrb[1][:, 0]]
                krb = [qkrb[0][:, 1], qkrb[1][:, 1]]
                t1w = two("t1w", [128, 2, 4, 64], F32)
                t2w = two("t2w", [128, 2, 4, 64], F32)
                kd = two("kd", [128, 4, 128], BF16)
                vb = two("vb", [128, 4, 128], BF16, 3)
                t1 = two("t1", [128, 4, 64], F32)
                t2 = two("t2", [128, 4, 64], F32)
                qkT = two("qkT", [128, 2, 4, 128], BF16, 3)
                STb = two("STb", [128, 4, 128], BF16)
                intra = two("intra", [128, 4, 128], F32)
                tmpc = two("tmpc", [128, 4, 128], F32)
                osb = two("osb", [128, 4, 128], F32)
                ynb = two("ynb", [128, 4, 128], BF16)
                stats = two("stats", [128, 4, 6], F32)
                mv = two("mv", [128, 4, 2], F32)
                rs = two("rs", [128, 4], F32)
                sg = two("sg", [128, 4, 128], F32, 4)
                tg = two("tg", [128, 4, 128], F32)
                ostr = two("ostr", [128, 4, 128], BF16)
                for i, c0 in ((0, 1544), (3, 3080)):
                    S.dma("pool", wr_[i], win_v[:, :, c0:c0 + 512], wr=[("wr", i)], key=f"wr{i}")
                S.dma("sp", ropeP, rope_prev, wr=["ropeP"], key="rope_1")
                S.dma("sp", ropeO, rope_own, wr=["ropeO"], key="rope_2")
                S.dma("sp", decT, decT_in, wr=["decT"], key="rope_3")
                for hh in range(4):
                    S.op("dve", lambda e, hh=hh: e.memset(Ctab[:, hh, :], GAMMA[hh] ** 128), wr=["Ctab"])

                def bc(ap2, n=128):
                    return ap2.unsqueeze(2).broadcast_to([128, 4, n])

                def rotary(bank, dst, dkey, rope, rkey, j, p):
                    X = PS[:, bank, :].rearrange("p (h z d) -> p h z d", z=2, d=64)
                    a = X[:, :, 0, :]
                    b = X[:, :, 1, :]
                    cs = rope[:, j, 0:64].unsqueeze(1).broadcast_to([128, 4, 64])
                    sn = rope[:, j, 64:128].unsqueeze(1).broadcast_to([128, 4, 64])
                    T1, T2 = t1[p], t2[p]
                    k1, k2 = ("t1", p), ("t2", p)
                    S.op("dve", lambda e: e.tensor_tensor(out=T1, in0=a, in1=cs, op=ALU.mult), rd=[("ps", bank), rkey], wr=[k1])
                    S.op("dve", lambda e: e.tensor_tensor(out=T2, in0=b, in1=sn, op=ALU.mult), rd=[("ps", bank), rkey], wr=[k2])
                    S.op("dve", lambda e: e.tensor_tensor(out=dst[:, :, 0:64], in0=T1, in1=T2, op=ALU.subtract),
                         rd=[k1, k2], wr=[(dkey, 0)])
                    S.op("dve", lambda e: e.tensor_tensor(out=T1, in0=a, in1=sn, op=ALU.mult), rd=[("ps", bank), rkey], wr=[k1])
                    S.op("dve", lambda e: e.tensor_tensor(out=T2, in0=b, in1=cs, op=ALU.mult), rd=[("ps", bank), rkey], wr=[k2])
                    S.op("dve", lambda e: e.tensor_tensor(out=dst[:, :, 64:128], in0=T1, in1=T2, op=ALU.add),
                         rd=[k1, k2], wr=[(dkey, 1)])

                def proj(bank, src, tsl, wi):
                    S.pe([lambda e, dc=dc: e.matmul(PS[:, bank, :], lhsT=src[:, dc, tsl], rhs=wr_[wi][:, dc, :],
                                                    start=(dc == 0), stop=(dc == 7)) for dc in range(8)],
                         rd=[("wr", wi)], wr=[("ps", bank)])

                def s1_proj(j):
                    p = j % 2
                    bk, bv = (0, 1) if p == 0 else (2, 3)
                    tsl = slice(j * 128, (j + 1) * 128)
                    proj(bk, hprev, tsl, 1)
                    proj(bv, hprev, tsl, 2)

                def s1_rest(j):
                    p = j % 2
                    bk, bv = (0, 1) if p == 0 else (2, 3)
                    rotary(bk, krb[p], ("krb", p), ropeP, "ropeP", j, p)
                    S.op("act", lambda e: e.activation(
                        out=vb[p], in_=PS[:, bv, :].rearrange("p (h d) -> p h d", d=128), func=AF.Copy),
                        rd=[("ps", bv)], wr=[("vb", p)])
                    S.op("pool", lambda e: e.tensor_tensor(
                        out=kd[p], in0=krb[p], in1=bc(qkdec[:, 8 + j * 4:12 + j * 4]), op=ALU.mult),
                        rd=[(("krb", p), 0), (("krb", p), 1), "qkdec"], wr=[("kd", p)])
                    S.pe([lambda e, hh=hh: e.matmul(
                        PS[:, 4, hh * 128:(hh + 1) * 128], lhsT=kd[p][:, hh, :], rhs=vb[p][:, hh, :],
                        start=(j == 0 and hh == 0), stop=(j == 15), skip_group_check=True) for hh in range(4)],
                        rd=[("kd", p), ("vb", p)], wr=[("ps", 4)])

                s1_proj(0)
                for j in range(16):
                    if j + 1 < 16:
                        s1_proj(j + 1)
                    s1_rest(j)
                S.op("dve", lambda e: e.tensor_scalar(out=St, in0=PS[:, 4, :].rearrange("p (h d) -> p h d", d=128),
                                                      scalar1=gs[:, 0:1], scalar2=None, op0=ALU.mult),
                     rd=[("ps", 4), "gs"], wr=["St"])

                def stageA(j):
                    p = j % 2
                    h3 = j % 3
                    tsl = slice(j * 128, (j + 1) * 128)
                    proj(0, hT, tsl, 0)
                    proj(1, hT, tsl, 1)
                    proj(2, hT, tsl, 2)
                    S.pe([lambda e, dc=dc, hh=hh: e.matmul(PS[:, 3, hh * 128:(hh + 1) * 128],
                                                           lhsT=wr_[3][:, dc, hh * 128:(hh + 1) * 128],
                                                           rhs=hT[:, dc, tsl], start=(dc == 0), stop=(dc == 7))
                          for hh in range(4) for dc in range(8)], rd=[("wr", 3)], wr=[("ps", 3)])
                    X2 = PS[:, 0:2, :].rearrange("p b (h z d) -> p b h z d", z=2, d=64)
                    a2 = X2[:, :, :, 0, :]
                    b2 = X2[:, :, :, 1, :]
                    cs2 = ropeO[:, j, 0:64].unsqueeze(1).unsqueeze(1).broadcast_to([128, 2, 4, 64])
                    sn2 = ropeO[:, j, 64:128].unsqueeze(1).unsqueeze(1).broadcast_to([128, 2, 4, 64])
                    T1, T2 = t1w[p], t2w[p]
                    kq = [(("qrb", p), 0), (("krb", p), 0)]
                    kq1 = [(("qrb", p), 1), (("krb", p), 1)]
                    pk = [("ps", 0), ("ps", 1), "ropeO"]
                    S.op("dve", lambda e: e.tensor_tensor(out=T1, in0=a2, in1=cs2, op=ALU.mult), rd=pk, wr=[("t1w", p)])
                    S.op("dve", lambda e: e.tensor_tensor(out=T2, in0=b2, in1=sn2, op=ALU.mult), rd=pk, wr=[("t2w", p)])
                    S.op("dve", lambda e: e.tensor_tensor(out=qkrb[p][:, :, :, 0:64], in0=T1, in1=T2, op=ALU.subtract),
                         rd=[("t1w", p), ("t2w", p)], wr=kq)
                    S.op("dve", lambda e: e.tensor_tensor(out=T1, in0=a2, in1=sn2, op=ALU.mult), rd=pk, wr=[("t1w", p)])
                    S.op("dve", lambda e: e.tensor_tensor(out=T2, in0=b2, in1=cs2, op=ALU.mult), rd=pk, wr=[("t2w", p)])
                    S.op("dve", lambda e: e.tensor_tensor(out=qkrb[p][:, :, :, 64:128], in0=T1, in1=T2, op=ALU.add),
                         rd=[("t1w", p), ("t2w", p)], wr=kq1)
                    S.op("act", lambda e: e.activation(out=vb[h3], in_=PS[:, 2, :].rearrange("p (h d) -> p h d", d=128), func=AF.Copy),
                         rd=[("ps", 2)], wr=[("vb", h3)])
                    S.op("act", lambda e: e.activation(out=sg[j % 4], in_=PS[:, 3, :].rearrange("p (h d) -> p h d", d=128), func=AF.Silu),
                         rd=[("ps", 3)], wr=[("sg", j % 4)])
                    S.op("act", lambda e: e.activation(out=dmy[:, 2:3], in_=dmy[:, 3:4], func=AF.Sqrt), wr=["dmy1"])
                    S.op("pool", lambda e: e.tensor_tensor(out=kd[p], in0=krb[p], in1=bc(qkdec[:, 4:8]), op=ALU.mult),
                         rd=[(("krb", p), 0), (("krb", p), 1), "qkdec"], wr=[("kd", p)])

                def stageA2(j):
                    p = j % 2
                    h3 = j % 3
                    p4 = PS[:, 4, :].bitcast(BF16)
                    S.pe([lambda e, hh=hh: e.transpose(out=p4[:, hh * 128:(hh + 1) * 128], in_=qrb[p][:, hh, :], identity=ident_b)
                          for hh in range(4)] +
                         [lambda e, hh=hh: e.transpose(out=p4[:, 512 + hh * 128:512 + (hh + 1) * 128], in_=krb[p][:, hh, :],
                                                       identity=ident_b) for hh in range(4)],
                         rd=[(("qrb", p), 0), (("qrb", p), 1), (("krb", p), 0), (("krb", p), 1), "ident_b"], wr=[("ps", 4)])
                    S.op("act", lambda e: e.activation(out=qkT[h3], in_=p4.rearrange("p (z h d) -> p z h d", z=2, d=128), func=AF.Copy),
                         rd=[("ps", 4)], wr=[("qkT", h3)])
                    S.op("act", lambda e: e.activation(out=Sbf[h3], in_=St, func=AF.Copy), rd=["St"], wr=[("Sbf", h3)])
                    S.pe([lambda e, hh=hh: e.matmul(PS[:, 3, hh * 128:(hh + 1) * 128], lhsT=kd[p][:, hh, :], rhs=vb[h3][:, hh, :],
                                                    start=True, stop=True) for hh in range(4)],
                         rd=[("kd", p), ("vb", h3)], wr=[("ps", 3)])
                    S.op("pool", lambda e: e.tensor_tensor(out=Stmp, in0=St, in1=Ctab, op=ALU.mult),
                         rd=["St", "Ctab"], wr=["Stmp"])
                    S.op("dve", lambda e: e.tensor_tensor(out=St, in0=Stmp, in1=PS[:, 3, :].rearrange("p (h d) -> p h d", d=128),
                                                          op=ALU.add), rd=["Stmp", ("ps", 3)], wr=["St"])

                def stageB(j):
                    p = j % 2
                    h3 = j % 3
                    qT = qkT[h3][:, 0]
                    kT = qkT[h3][:, 1]
                    S.pe([lambda e, hh=hh: e.matmul(PS[:, 5, hh * 128:(hh + 1) * 128], lhsT=kT[:, hh, :], rhs=qT[:, hh, :],
                                                    start=True, stop=True) for hh in range(4)],
                         rd=[("qkT", h3)], wr=[("ps", 5)])
                    S.op("dve", lambda e: e.tensor_tensor(out=STb[p], in0=PS[:, 5, :].rearrange("p (h d) -> p h d", d=128), in1=decT,
                                                          op=ALU.mult), rd=[("ps", 5), "decT"], wr=[("STb", p)])
                    S.pe([lambda e, hh=hh: e.matmul(PS[:, 6, hh * 128:(hh + 1) * 128], lhsT=STb[p][:, hh, :], rhs=vb[h3][:, hh, :],
                                                    start=True, stop=True) for hh in range(4)],
                         rd=[("STb", p), ("vb", h3)], wr=[("ps", 6)])
                    S.pe([lambda e, hh=hh: e.matmul(PS[:, 5, hh * 128:(hh + 1) * 128], lhsT=qT[:, hh, :], rhs=Sbf[h3][:, hh, :],
                                                    start=True, stop=True) for hh in range(4)],
                         rd=[("qkT", h3), ("Sbf", h3)], wr=[("ps", 5)])
                    S.op("act", lambda e: e.activation(out=intra[p], in_=PS[:, 6, :].rearrange("p (h d) -> p h d", d=128), func=AF.Copy),
                         rd=[("ps", 6)], wr=[("intra", p)])
                    for hh in range(4):
                        S.op("act", lambda e, hh=hh: e.activation(out=tmpc[p][:, hh, :], in_=PS[:, 5, hh * 128:(hh + 1) * 128],
                                                                  func=AF.Copy, scale=qkdec[:, hh:hh + 1]),
                             rd=[("ps", 5), "qkdec"], wr=[("tmpc", p, hh)])
                    S.op("dve", lambda e: e.tensor_tensor(out=osb[p], in0=tmpc[p], in1=intra[p], op=ALU.add),
                         rd=[("tmpc", p, hh) for hh in range(4)] + [("intra", p)], wr=[("osb", p)])
                    for hh in range(4):
                        S.op("dve", lambda e, hh=hh: e.bn_stats(out=stats[p][:, hh, :], in_=osb[p][:, hh, :]),
                             rd=[("osb", p)], wr=[("stats", p, hh)])
                        S.op("dve", lambda e, hh=hh: e.bn_aggr(out=mv[p][:, hh, :], in_=stats[p][:, hh, :]),
                             rd=[("stats", p, hh)], wr=[("mv", p, hh)])
                    S.op("act", lambda e: e.activation(out=rs[p], in_=mv[p][:, :, 1], func=AF.Sqrt, bias=epsc[:, 0:1], scale=1.0),
                         rd=[("mv", p, hh) for hh in range(4)] + ["epsc"], wr=[("rs", p)])
                    S.op("act", lambda e: e.activation(out=dmy[:, 0:1], in_=dmy[:, 1:2], func=AF.Silu), wr=["dmy0"])
                    S.op("dve", lambda e: e.reciprocal(out=rs[p], in_=rs[p]), rd=[("rs", p)], wr=[("rs", p)])
                    S.op("dve", lambda e: e.tensor_tensor(out=osb[p], in0=osb[p], in1=bc(mv[p][:, :, 0]), op=ALU.subtract),
                         rd=[("osb", p)] + [("mv", p, hh) for hh in range(4)], wr=[("osb", p)])
                    S.op("dve", lambda e: e.tensor_tensor(out=ynb[p], in0=osb[p], in1=bc(rs[p]), op=ALU.mult),
                         rd=[("osb", p), ("rs", p)], wr=[("ynb", p)])

                def stageC(j):
                    p = j % 2
                    h3 = j % 3
                    tsl = slice(j * 128, (j + 1) * 128)
                    p7 = PS[:, 7, :].bitcast(BF16)
                    S.pe([lambda e, hh=hh: e.transpose(out=p7[:, hh * 128:(hh + 1) * 128], in_=ynb[p][:, hh, :], identity=ident_b)
                          for hh in range(4)], rd=[("ynb", p), "ident_b"], wr=[("ps", 7)])
                    for hh in range(4):
                        S.op("act", lambda e, hh=hh: e.activation(out=tg[p][:, hh, :], in_=p7[:, hh * 128:(hh + 1) * 128],
                                                                  func=AF.Identity, scale=gnv[:, hh:hh + 1], bias=gnv[:, 4 + hh:5 + hh]),
                             rd=[("ps", 7), "gnv"], wr=[("tg", p, hh)])
                    S.op("dve", lambda e: e.tensor_tensor(out=ostr[p], in0=tg[p], in1=sg[j % 4], op=ALU.mult),
                         rd=[("tg", p, hh) for hh in range(4)] + [("sg", j % 4)], wr=[("ostr", p)])
                    S.dma("sp", oT_d[:, 4:8, tsl], ostr[p], rd=[("ostr", p)], wr=[("oTd_r", j)], key=f"qaug0_{2 + p}")

                stageA(0)
                stageA2(0)
                stageA(1)
                stageA2(1)
                for j in range(16):
                    if j + 2 < 16:
                        stageA(j + 2)
                    stageB(j)
                    if j + 2 < 16:
                        stageA2(j + 2)
                    if j >= 1:
                        stageC(j - 1)
                stageC(15)
            S.barrier()


        Q = contextlib.ExitStack()
        xT2 = sb("xT2", [128, 8, NT], F32, Q)
        PM = contextlib.ExitStack()
        oT = sb("oTr", [128, 8, NT], BF16, PM)
        wo = sb("wo", [128, 8, D], BF16, PM)
        wo_v = w_o.rearrange("(dc p) f -> p dc f", p=128)
        for dch in range(8):
            S.dma("pool", wo[:, :, dch * 128:(dch + 1) * 128], wo_v[:, :, dch * 128:(dch + 1) * 128], wr=[("wo", dch)],
                  key=f"wo{dch}")
        for tt in range(4):
            ts = slice(tt * 512, (tt + 1) * 512)
            dep = [("wo", dch) for dch in range(8)] if tt else []
            S.dma("sp", oT[:, :, ts], oT_d[:, :, ts], rd=dep, wr=[("oTr", tt)], key=f"x2sp{tt}")
            S.dma("sp", xT2[:, :, ts], x1sp[:, :, ts], rd=dep, wr=[("xT", dc, tt) for dc in range(8)], key=f"x1ld{tt}")
        lnb2 = ln_bufs(PM)
        wocnt = [0]

        def before2(tt):
            ts = slice(tt * 512, (tt + 1) * 512)
            for dch in range(8):
                bank = wocnt[0] % 2
                wocnt[0] += 1
                S.pe([lambda e, c=c, dch=dch, ts=ts, bank=bank: e.matmul(
                    PS[:, bank, :], lhsT=wo[:, c, dch * 128:(dch + 1) * 128], rhs=oT[:, c, ts], start=(c == 0), stop=(c == 7))
                    for c in range(8)], rd=[("wo", dch), ("oTr", tt)], wr=[("ps", bank)])
                S.op("dve", lambda e, dch=dch, ts=ts, bank=bank: e.scalar_tensor_tensor(
                    out=xT2[:, dch, ts], in0=PS[:, bank, :], scalar=V(G2, dch), in1=xT2[:, dch, ts],
                    op0=ALU.mult, op1=ALU.add), rd=[("ps", bank), ("vec", G2), ("xT", dch, tt)], wr=[("xT", dch, tt)])
        ln_all(xT2, AG2, AB2, A3, B3, lnb2, before=before2)
        if DEBUG == "x2":
            S.dma("sp", dbg, xT2, rd=[("xT", dc, tt) for dc in range(8) for tt in range(4)], wr=["dbg"], key="dbg")
        S.barrier()
        PM.close()
        if DEBUG == "x2":
            return nc

        with contextlib.ExitStack() as P3:
            xT3 = xT2
            lnb3 = ln_bufs(P3, nrot=2)
            otok = [sb(f"otok{i}", [128, D], F32, P3) for i in range(2)]
            def after3(tt):
                for b4 in range(4):
                    blk = tt * 4 + b4
                    sl = ocnt[0] % 2
                    ocnt[0] += 1
                    b0 = 2 * sl
                    pv = PS[:, b0:b0 + 2, :].rearrange("p b (k c) -> p (b k) c", c=128)
                    S.pe([lambda e, dc=dc, blk=blk, pv=pv: e.transpose(out=pv[:, dc, :], in_=xT3[:, dc, blk * 128:(blk + 1) * 128],
                                                                      identity=ident_f) for dc in range(8)],
                         rd=[("xT", dc, tt) for dc in range(8)] + ["ident_f"], wr=[("ps", b0), ("ps", b0 + 1)])
                    for hb in range(2):
                        eng = "act" if hb else "dve"
                        ov = otok[sl].rearrange("p (k c) -> p k c", c=128)[:, hb * 4:hb * 4 + 4, :]
                        if hb:
                            S.op("act", lambda e, ov=ov, pv=pv: e.activation(out=ov, in_=pv[:, 4:8, :], func=AF.Copy),
                                 rd=[("ps", b0 + 1)], wr=[("otok", sl, 1)])
                        else:
                            S.op("dve", lambda e, ov=ov, pv=pv: e.tensor_copy(out=ov, in_=pv[:, 0:4, :]),
                                 rd=[("ps", b0)], wr=[("otok", sl, 0)])
                    S.dma("sp", out[blk * 128:(blk + 1) * 128, :], otok[sl], rd=[("otok", sl, 0), ("otok", sl, 1)],
                          wr=[("out", blk)], key=f"out{sl}")
            ocnt = [0]

            def ap3(tt):
                ln_apply(xT3, tt, 13, 14, None, None, lnb3)
                after3(tt)
            ffn(P3, f2g, f2u, f2d, G3H, xT3, ln_cb=(lambda tt: ln_stats(xT3, tt, lnb3), ap3))
        S.barrier()
        Q.close()
    return nc


def _host_consts(g):
    ident = np.eye(128, dtype=np.float32)
    s = np.arange(128)
    maskT = np.where(s[:, None] <= s[None, :], 0.0, NEG).astype(np.float32)
    lg = np.log1p(-np.power(np.float32(2.0), -5.0 - np.arange(RET_H, dtype=np.float32))).astype(np.float32)
    idx = np.arange(128, dtype=np.float32)
    diff = idx[None, :] - idx[:, None]
    decT = np.where(diff >= 0, np.exp(lg[:, None, None] * np.maximum(diff, 0.0)), 0.0).astype(np.float32)
    decT = (decT * np.float32(128.0 ** -0.5)).transpose(1, 0, 2).copy()
    qdec = np.exp(lg[None, :] * (idx[:, None] + 1.0)).astype(np.float32)
    kdec = (np.exp(lg[None, :] * (127.0 - idx[:, None])) * np.float32(128.0 ** -0.5)).astype(np.float32)
    cpow = np.exp(lg[None, :] * 128.0 * (15.0 - np.arange(16, dtype=np.float32))[:, None]).astype(np.float32)
    kdecJ = (kdec[:, None, :] * cpow[None, :, :]).reshape(128, 64).astype(np.float32)
    qkdec = np.concatenate([qdec, kdec, kdecJ], axis=1).astype(np.float32)
    inv_freq = (np.float32(10000.0) ** (-np.arange(64, dtype=np.float32) / np.float32(64))).astype(np.float32)

    def rope(p0):
        pos = (p0 + np.arange(2048)).astype(np.float32)
        ang = (pos[:, None] * inv_freq[None, :]).astype(np.float32)
        t = np.concatenate([np.cos(ang), np.sin(ang)], axis=1).astype(np.float32)
        return t.reshape(16, 128, 128).transpose(1, 0, 2).copy()

    return dict(ident=ident, maskT=maskT, decT=decT, qkdec=qkdec, rope_prev=rope(0), rope_own=rope(g * 2048))


def kernel(**inputs):
    f = lambda a: np.ascontiguousarray(np.asarray(a, dtype=np.float32))
    x = f(inputs["x"]); c = f(inputs["c"])

    def fm(v, n):
        return np.ascontiguousarray(f(v).reshape(n, 128).T)

    lnp = np.concatenate([fm(inputs[k][0], 8) for k in ("ln1_g", "ln1_b", "ln2_g", "ln2_b", "ln3_g", "ln3_b")], axis=1)
    gnp = np.concatenate([fm(inputs["ret_gn_g"][0], 4), fm(inputs["ret_gn_b"][0], 4)], axis=1)
    shared = {
        "w_ada": f(inputs["w_ada"][0]), "b_ada": fm(inputs["b_ada"][0], 72),
        "ffn1_w_gate": f(inputs["ffn1_w_gate"][0]), "ffn1_w_up": f(inputs["ffn1_w_up"][0]),
        "ffn1_w_down": f(inputs["ffn1_w_down"][0]),
        "ffn2_w_gate": f(inputs["ffn2_w_gate"][0]), "ffn2_w_up": f(inputs["ffn2_w_up"][0]),
        "ffn2_w_down": f(inputs["ffn2_w_down"][0]),
        "lnp": np.ascontiguousarray(lnp), "w_in": f(inputs["w_in"][0]),
        "fox_b_f": f(inputs["fox_b_f"][0]).reshape(8, 1), "gnp": np.ascontiguousarray(gnp), "w_o": f(inputs["w_o"][0]),
    }
    in_maps = []
    for core in range(8):
        b, g = core // 2, core % 2
        m = dict(shared)
        m["x"] = np.ascontiguousarray(x[b, g * NT:(g + 1) * NT, :])
        m["cT"] = fm(c[b], 8)
        gs = np.zeros((128, 3), np.float32)
        gs[:, 0] = float(g)
        gs[:, 1] = NEG * (1.0 - g)
        gs[:, 2] = np.where((np.arange(128) % 16) < 8, NEG * (1.0 - g), 0.0)
        m["gsel"] = gs
        m.update(_host_consts(g))
        in_maps.append(m)
    nc = build_nc()
    res = run_bass_kernel_spmd(nc, in_maps, core_ids=list(range(8)))
    outp = np.empty((4, 4096, D), np.float32)
    for core in range(8):
        b, g = core // 2, core % 2
        outp[b, g * NT:(g + 1) * NT, :] = res.results[core]["out"]
    if DEBUG is not None:
        kernel.dbg = [res.results[core]["dbg"] for core in range(8)]
    return outp
```

```python
import contextlib
import numpy as np
import concourse.bass as bass
import concourse.mybir as mybir
from concourse.bass_utils import run_bass_kernel_spmd

F32 = mybir.dt.float32
BF16 = mybir.dt.bfloat16
AF = mybir.ActivationFunctionType
ALU = mybir.AluOpType

D = 1024
DFF = 2816
NT = 2048
ALPHA = 2.0 ** 0.25
EPS = 1e-5
NEG = -30000.0
RET_H = 4
GAMMA = [float(np.exp(np.float32(np.log1p(-np.float32(2.0) ** np.float32(-5.0 - h))))) for h in range(RET_H)]
DEBUG = None


class Sched:
    def __init__(self, nc):
        self.nc = nc
        self.eng = {"pe": nc.tensor, "act": nc.scalar, "dve": nc.vector, "pool": nc.gpsimd, "sp": nc.sync}
        self.sem = {e: nc.alloc_semaphore("s_" + e) for e in ("pe", "act", "dve", "pool")}
        self.cnt = {e: 0 for e in self.sem}
        self.dsem = {}
        self.dcnt = {}
        self.lastw = {}
        self.readers = {}
        self.waited = {e: {} for e in self.eng}

    def _semof(self, tok):
        return self.sem[tok[0]] if tok[0] in self.sem else self.dsem[tok[0]]

    @staticmethod
    def _excl(rd, wr):
        ps = [k for k in rd if isinstance(k, tuple) and k[0] == "ps"]
        return [k for k in rd if not (isinstance(k, tuple) and k[0] == "ps")], list(wr) + ps

    def _waits(self, e, rd, wr):
        rd, wr = self._excl(rd, wr)
        deps = set()
        for k in rd:
            if k in self.lastw:
                deps.add(self.lastw[k])
        for k in wr:
            if k in self.lastw:
                deps.add(self.lastw[k])
            deps |= self.readers.get(k, set())
        need = {}
        for (se, c) in deps:
            if se == e and e == "pe":
                continue
            if self.waited[e].get(se, 0) >= c:
                continue
            need[se] = max(need.get(se, 0), c)
        for se, c in need.items():
            self.eng[e].wait_ge(self._semof((se, c)), c)
            self.waited[e][se] = c

    def _reg(self, tok, rd, wr):
        rd, wr = self._excl(rd, wr)
        for k in wr:
            self.lastw[k] = tok
            self.readers[k] = set()
        for k in rd:
            self.readers.setdefault(k, set()).add(tok)

    def op(self, e, fn, rd=(), wr=()):
        self._waits(e, rd, wr)
        ins = fn(self.eng[e])
        self.cnt[e] += 1
        ins.then_inc(self.sem[e], 1)
        self._reg((e, self.cnt[e]), rd, wr)

    def pe(self, fns, rd=(), wr=()):
        self._waits("pe", rd, wr)
        ins = None
        for f in fns:
            ins = f(self.nc.tensor)
        self.cnt["pe"] += 1
        ins.then_inc(self.sem["pe"], 1)
        self._reg(("pe", self.cnt["pe"]), rd, wr)

    def dma(self, q, out, in_, rd=(), wr=(), key=None):
        assert key is not None
        dk = "d_" + key
        if dk not in self.dsem:
            self.dsem[dk] = self.nc.alloc_semaphore(dk)
            self.dcnt[dk] = 0
        self._waits(q, rd, wr)
        self.eng[q].dma_start(out=out, in_=in_).then_inc(self.dsem[dk], 16)
        self.dcnt[dk] += 16
        self._reg((dk, self.dcnt[dk]), rd, wr)

    def custom(self, e, fn, semname, inc, rd=(), wr=()):
        dk = "c_" + semname
        if dk not in self.dsem:
            self.dsem[dk] = self.nc.alloc_semaphore(dk)
            self.dcnt[dk] = 0
        self._waits(e, rd, wr)
        fn(self.eng[e]).then_inc(self.dsem[dk])
        self.dcnt[dk] += inc
        self._reg((dk, self.dcnt[dk]), rd, wr)

    def barrier(self, skip_prefix=None, keep=()):
        kept = {k: self.lastw[k] for k in keep if k in self.lastw}
        for e in self.eng:
            for se, c in self.cnt.items():
                if se != e and c > self.waited[e].get(se, 0):
                    self.eng[e].wait_ge(self.sem[se], c)
                    self.waited[e][se] = c
            for dk, c in self.dcnt.items():
                if skip_prefix is not None and dk.startswith(skip_prefix):
                    continue
                if c > self.waited[e].get(dk, 0):
                    self.eng[e].wait_ge(self.dsem[dk], c)
                    self.waited[e][dk] = c
        for e in ("act", "dve", "pool"):
            c = self.cnt[e]
            if c > self.waited[e].get(e, 0):
                self.eng[e].wait_ge(self.sem[e], c)
                self.waited[e][e] = c
        self.lastw = dict(kept)
        self.readers = {}


def build_nc():
    nc = bass.Bass("TRN2", target_bir_lowering=False)
    S = Sched(nc)

    def din(name, shape):
        return nc.dram_tensor(name, list(shape), F32, kind="ExternalInput").ap()

    x = din("x", [NT, D])
    cT = din("cT", [128, 8])
    gsel = din("gsel", [128, 3])
    w_ada = din("w_ada", [D, 9 * D])
    b_ada = din("b_ada", [128, 72])
    f1g = din("ffn1_w_gate", [D, DFF]); f1u = din("ffn1_w_up", [D, DFF]); f1d = din("ffn1_w_down", [DFF, D])
    f2g = din("ffn2_w_gate", [D, DFF]); f2u = din("ffn2_w_up", [D, DFF]); f2d = din("ffn2_w_down", [DFF, D])
    lnp = din("lnp", [128, 48])
    w_in = din("w_in", [D, 3592])
    fbf = din("fox_b_f", [8, 1])
    gnp = din("gnp", [128, 8])
    w_o = din("w_o", [D, D])
    rope_prev = din("rope_prev", [128, 16, 128])
    rope_own = din("rope_own", [128, 16, 128])
    decT_in = din("decT", [128, 4, 128])
    qkdec_in = din("qkdec", [128, 72])
    mask_in = din("maskT", [128, 128])
    ident_in = din("ident", [128, 128])
    out = nc.dram_tensor("out", [NT, D], F32, kind="ExternalOutput").ap()
    x1sp = nc.dram_tensor("x1sp", [128, 8, NT], F32).ap()
    qaug_d = nc.dram_tensor("qaug_d", [8, 6, NT], BF16).ap()
    kaug_d = nc.dram_tensor("kaug_d", [8, 3, 2 * NT], BF16).ap()
    cdram = nc.dram_tensor("cdram", [8, 2 * NT], F32).ap()
    oT_d = nc.dram_tensor("oT_d", [128, 8, NT], BF16).ap()
    ibs = [nc.dram_tensor(f"ib{q}", [512, 1024], BF16) for q in range(4)]
    obs = [nc.dram_tensor(f"ob{q}", [1024, 1024], BF16) for q in range(4)]
    dbg = None
    if DEBUG is not None:
        dbg = nc.dram_tensor("dbg", [128, 8, NT], F32, kind="ExternalOutput").ap()

    PS = nc.alloc_psum_tensor("ps", [128, 8, 512], F32).ap()

    def psb(b):
        return PS[:, b, :]

    with contextlib.ExitStack() as G:
        _ctr = [0]

        def sb(name, shape, dt, st=G):
            _ctr[0] += 1
            return st.enter_context(nc.sbuf_tensor(f"{name}_{_ctr[0]}", list(shape), dt)).ap()

        ident_f = sb("ident_f", [128, 128], F32)
        ident_b = sb("ident_b", [128, 128], BF16)
        ones_b = sb("ones_b", [128, 128], BF16)
        mask_f = sb("mask_f", [128, 128], F32)
        mask_b = sb("mask_b", [128, 128], BF16)
        gs = sb("gs", [128, 3], F32)
        modfm = sb("modfm", [128, 72], F32)
        badaf = sb("badaf", [128, 72], F32)
        lnv = sb("lnv", [128, 48], F32)
        vec = sb("vec", [128, 16, 8], F32)
        gnv = sb("gnv", [128, 8], F32)
        qkdec = sb("qkdec_s", [128, 72], F32)
        ctile = sb("ctile", [128, 8], F32)
        cact = sb("cact", [128, 8], BF16)
        fb = sb("fb", [8, 1], F32)
        hT = sb("hT", [128, 8, NT], BF16)

        SC1P, SH1, G1H, A2, B2, G2, A3, B3, G3H, AG1, AB1, AG2, AB2 = range(13)

        def V(i, dc):
            return vec[:, i, dc:dc + 1]

        S.dma("sp", ident_f, ident_in, wr=["ident_f"], key="c0_1")
        S.dma("sp", mask_f, mask_in, wr=["mask_f"], key="c0_2")
        S.dma("sp", gs, gsel, wr=["gs"], key="c0_3")
        S.dma("sp", badaf, b_ada, wr=["badaf"], key="c0_4")
        S.dma("sp", lnv, lnp, wr=["lnv"], key="c0_5")
        S.dma("sp", gnv, gnp, wr=["gnv"], key="c0_6")
        S.dma("sp", qkdec, qkdec_in, wr=["qkdec"], key="c0_7")
        S.dma("sp", ctile, cT, wr=["ctile"], key="c0_8")
        S.dma("sp", fb, fbf, wr=["fb"], key="c0_9")
        S.op("dve", lambda e: e.tensor_copy(out=ident_b, in_=ident_f), rd=["ident_f"], wr=["ident_b"])
        S.op("dve", lambda e: e.tensor_copy(out=mask_b, in_=mask_f), rd=["mask_f"], wr=["mask_b"])
        S.op("dve", lambda e: e.memset(ones_b, 1.0), wr=["ones_b"])
        epsc = sb("epsc", [128, 1], F32)
        S.op("dve", lambda e: e.memset(epsc, EPS), wr=["epsc"])
        S.op("act", lambda e: e.activation(out=cact, in_=ctile, func=AF.Silu), rd=["ctile"], wr=["cact"])

        wa_v = w_ada.rearrange("(dc p) f -> p dc f", p=128)

        with contextlib.ExitStack() as P1:
            xT = sb("xT", [128, 8, NT], F32, P1)

            def M(i):
                return modfm[:, i * 8:(i + 1) * 8]

            def L(i):
                return lnv[:, i * 8:(i + 1) * 8]

            def mod_dma(v, buf, bkey):
                S.dma("pool", buf, wa_v[:, :, v * 1024:(v + 1) * 1024], wr=[bkey], key=f"wabd_{bkey}")

            def mod_mm(v, buf, bkey):
                fns = []
                for k in range(8):
                    for dc in range(8):
                        fns.append(lambda e, k=k, dc=dc: e.matmul(
                            PS[:, 6, k:k + 1], lhsT=buf[:, dc, k * 128:(k + 1) * 128],
                            rhs=cact[:, dc:dc + 1], start=(dc == 0), stop=(dc == 7)))
                S.pe(fns, rd=[bkey, "cact"], wr=[("ps", 6)])
                S.op("dve", lambda e: e.tensor_tensor(out=M(v), in0=PS[:, 6, 0:8], in1=badaf[:, v * 8:(v + 1) * 8], op=ALU.add),
                     rd=[("ps", 6), "badaf"], wr=[("modfm", v)])

            def vop(fn, w, rm=(), rv=()):
                S.op("dve", fn, rd=[("modfm", m) for m in rm] + ["lnv"] + [("vec", r) for r in rv], wr=[("vec", w)])

            def vec_after(v):
                if v == 2:
                    vop(lambda e: e.tensor_scalar(out=vec[:, G1H, :], in0=M(2), scalar1=0.5, scalar2=None, op0=ALU.mult), G1H, rm=[2])
                if v == 4:
                    vop(lambda e: e.tensor_scalar(out=vec[:, 15, :], in0=M(4), scalar1=1.0, scalar2=None, op0=ALU.add), 15, rm=[4])
                    vop(lambda e: e.tensor_tensor(out=vec[:, A2, :], in0=L(0), in1=vec[:, 15, :], op=ALU.mult), A2, rv=[15])
                    vop(lambda e: e.tensor_tensor(out=vec[:, B2, :], in0=L(1), in1=vec[:, 15, :], op=ALU.mult), B2, rv=[15])
                    vop(lambda e: e.tensor_tensor(out=vec[:, B2, :], in0=vec[:, B2, :], in1=M(3), op=ALU.add), B2, rm=[3], rv=[B2])
                if v == 5:
                    vop(lambda e: e.tensor_copy(out=vec[:, G2, :], in_=M(5)), G2, rm=[5])
                if v == 7:
                    vop(lambda e: e.tensor_scalar(out=vec[:, 15, :], in0=M(7), scalar1=1.0, scalar2=None, op0=ALU.add), 15, rm=[7], rv=[15])
                    vop(lambda e: e.tensor_tensor(out=vec[:, A3, :], in0=L(2), in1=vec[:, 15, :], op=ALU.mult), A3, rv=[15])
                    vop(lambda e: e.tensor_tensor(out=vec[:, B3, :], in0=L(3), in1=vec[:, 15, :], op=ALU.mult), B3, rv=[15])
                    vop(lambda e: e.tensor_tensor(out=vec[:, B3, :], in0=vec[:, B3, :], in1=M(6), op=ALU.add), B3, rm=[6], rv=[B3])
                if v == 8:
                    vop(lambda e: e.tensor_scalar(out=vec[:, G3H, :], in0=M(8), scalar1=0.5, scalar2=None, op0=ALU.mult), G3H, rm=[8])

            wpc = [sb(f"wpc{i}", [128, 8, 256], BF16, P1) for i in range(2)]
            hstate = [0]

            def ffn1_hook():
                i = hstate[0]
                hstate[0] += 1
                if i < 28:
                    v, q = 2 + i // 4, i % 4
                    S.dma("pool", wpc[i % 2], wa_v[:, :, v * 1024 + q * 256: v * 1024 + (q + 1) * 256], wr=[("wpc", i % 2)],
                          key=f"wpc{i % 2}")
                n = i - 1
                if 0 <= n < 28:
                    v, q = 2 + n // 4, n % 4
                    buf = wpc[n % 2]
                    fns = []
                    for kk in range(2):
                        for dc in range(8):
                            fns.append(lambda e, kk=kk, dc=dc: e.matmul(
                                PS[:, 6, q * 2 + kk: q * 2 + kk + 1], lhsT=buf[:, dc, kk * 128:(kk + 1) * 128],
                                rhs=cact[:, dc:dc + 1], start=(dc == 0), stop=(dc == 7)))
                    S.pe(fns, rd=[("wpc", n % 2), "cact"], wr=[("ps", 6)])
                    if q == 3:
                        S.op("dve", lambda e: e.tensor_tensor(out=M(v), in0=PS[:, 6, 0:8], in1=badaf[:, v * 8:(v + 1) * 8], op=ALU.add),
                             rd=[("ps", 6), "badaf"], wr=[("modfm", v)])
                        vec_after(v)

            with contextlib.ExitStack() as P0:
                wab0 = sb("wab0", [128, 8, 1024], BF16, P0)
                wab1 = sb("wab1", [128, 8, 1024], BF16, P0)
                mod_dma(1, wab1, "wab1")
                mod_dma(0, wab0, "wab0")
                vop(lambda e: e.tensor_scalar(out=vec[:, AG1, :], in0=L(0), scalar1=ALPHA, scalar2=None, op0=ALU.mult), AG1)
                vop(lambda e: e.tensor_scalar(out=vec[:, AB1, :], in0=L(1), scalar1=ALPHA, scalar2=None, op0=ALU.mult), AB1)
                vop(lambda e: e.tensor_scalar(out=vec[:, AG2, :], in0=L(2), scalar1=ALPHA, scalar2=None, op0=ALU.mult), AG2)
                vop(lambda e: e.tensor_scalar(out=vec[:, AB2, :], in0=L(3), scalar1=ALPHA, scalar2=None, op0=ALU.mult), AB2)
                vop(lambda e: e.tensor_copy(out=vec[:, 13, :], in_=L(4)), 13)
                vop(lambda e: e.tensor_copy(out=vec[:, 14, :], in_=L(5)), 14)
                mod_mm(1, wab1, "wab1")
                mod_mm(0, wab0, "wab0")
                vop(lambda e: e.tensor_scalar(out=vec[:, SC1P, :], in0=M(1), scalar1=1.0, scalar2=None, op0=ALU.add), SC1P, rm=[1])
                vop(lambda e: e.tensor_copy(out=vec[:, SH1, :], in_=M(0)), SH1, rm=[0])

                xtok = [sb(f"xtok{i}", [128, D], F32, P0) for i in range(2)]
                for blk in range(16):
                    sl = blk % 2
                    S.dma("sp", xtok[sl], x[blk * 128:(blk + 1) * 128, :], wr=[("xtok", sl)], key=f"xtok{sl}")
                    b0 = 1 + 2 * sl
                    pv = PS[:, b0:b0 + 2, :].rearrange("p b (k c) -> p (b k) c", c=128)
                    S.pe([lambda e, dc=dc, sl=sl, pv=pv: e.transpose(out=pv[:, dc, :], in_=xtok[sl][:, dc * 128:(dc + 1) * 128],
                                                                    identity=ident_f) for dc in range(8)],
                         rd=[("xtok", sl), "ident_f"], wr=[("ps", b0), ("ps", b0 + 1)])
                    for hb in range(2):
                        S.op("dve", lambda e, pv=pv, blk=blk, hb=hb: e.tensor_scalar(
                            out=xT[:, hb * 4:hb * 4 + 4, blk * 128:(blk + 1) * 128], in0=pv[:, hb * 4:hb * 4 + 4, :],
                            scalar1=ALPHA, scalar2=None, op0=ALU.mult),
                            rd=[("ps", b0 + hb)], wr=[("xT", dc, blk // 4) for dc in range(hb * 4, hb * 4 + 4)])
                    for dc in range(8):
                        if dc < 4:
                            S.op("act", lambda e, pv=pv, dc=dc, blk=blk: e.activation(
                                out=hT[:, dc, blk * 128:(blk + 1) * 128], in_=pv[:, dc, :], func=AF.Identity,
                                scale=V(SC1P, dc), bias=V(SH1, dc)),
                                rd=[("ps", b0), ("vec", SC1P), ("vec", SH1)], wr=[("hT", dc, blk // 4)])
                        else:
                            S.op("dve", lambda e, pv=pv, dc=dc, blk=blk: e.tensor_scalar(
                                out=hT[:, dc, blk * 128:(blk + 1) * 128], in0=pv[:, dc, :],
                                scalar1=V(SC1P, dc), scalar2=V(SH1, dc), op0=ALU.mult, op1=ALU.add),
                                rd=[("ps", b0 + 1), ("vec", SC1P), ("vec", SH1)], wr=[("hT", dc, blk // 4)])

            if DEBUG == "p0b":
                for tt in range(4):
                    S.dma("sp", dbg[:, :, tt * 512:(tt + 1) * 512], xT[:, :, tt * 512:(tt + 1) * 512],
                          rd=[("xT", dc, tt) for dc in range(8)], wr=[("dbg", tt)], key=f"dbg{tt}")
                S.barrier()
                return nc

            def ffn(st, wg, wu, wd, GV, xTt, hook=None, ln_cb=None):
                aT = sb("aT", [128, 11, NT], BF16, st)
                wgu = [sb(f"wgu{i}", [128, 2, 8, 128], BF16, st) for i in range(2)]
                if ln_cb is None:
                    wdb = [sb(f"wdb{i}", [128, 11, 128], BF16, st) for i in range(2)]
                else:
                    wdall = sb("wdall", [128, 8, 11, 128], BF16, st)
                    wdb = [wdall[:, 0], wdall[:, 1]]
                nst = 2 if ln_cb is None else 1
                stmp = [sb(f"stmp{i}", [128, 512], F32, st) for i in range(nst)]
                wg_v = wg.rearrange("(dc p) f -> p dc f", p=128)
                wu_v = wu.rearrange("(dc p) f -> p dc f", p=128)
                wd_v = wd.rearrange("(fc p) d -> p fc d", p=128)
                ucnt = 0
                wcnt = 0
                dcnt = 0
                for half in range(2):
                    for fi in range(11):
                        fc = half * 11 + fi
                        if hook is not None:
                            hook()
                        sl = wcnt % 2
                        wcnt += 1
                        S.dma("pool", wgu[sl][:, 0], wg_v[:, :, fc * 128:(fc + 1) * 128], wr=[("wgu", sl)], key=f"wgu{sl}")
                        S.dma("pool", wgu[sl][:, 1], wu_v[:, :, fc * 128:(fc + 1) * 128], wr=[("wgu", sl)], key=f"wgu{sl}")
                        if ln_cb is not None and half == 1 and 1 <= fi <= 8:
                            dq = fi - 1
                            S.dma("pool", wdall[:, dq], wd_v[:, 11:22, dq * 128:(dq + 1) * 128],
                                  wr=[("wdall", dq)] + ([("wdb", dq)] if dq < 2 else []), key=f"wo{dq}")
                        for tt in range(4):
                            pb = (ucnt % 2) * 2
                            ts = slice(tt * 512, (tt + 1) * 512)
                            fns = []
                            for gu in range(2):
                                for dc in range(8):
                                    fns.append(lambda e, gu=gu, dc=dc, sl=sl, pb=pb, ts=ts: e.matmul(
                                        PS[:, pb + gu, :], lhsT=wgu[sl][:, gu, dc, :], rhs=hT[:, dc, ts],
                                        start=(dc == 0), stop=(dc == 7)))
                            S.pe(fns, rd=[("wgu", sl)] + [("hT", dc, tt) for dc in range(8)], wr=[("ps", pb), ("ps", pb + 1)])
                            st_i = ucnt % nst
                            S.op("act", lambda e, pb=pb, st_i=st_i: e.activation(out=stmp[st_i], in_=PS[:, pb, :], func=AF.Silu),
                                 rd=[("ps", pb)], wr=[("stmp", st_i)])
                            S.op("dve", lambda e, pb=pb, st_i=st_i, fi=fi, ts=ts: e.tensor_tensor(
                                out=aT[:, fi, ts], in0=stmp[st_i], in1=PS[:, pb + 1, :], op=ALU.mult),
                                rd=[("stmp", st_i), ("ps", pb + 1)], wr=[("aT", fi, tt)])
                            ucnt += 1
                    if ln_cb is not None and half == 1:
                        st_fn, ap_fn = ln_cb

                        def Dtile(tt):
                            nonlocal dcnt
                            ts = slice(tt * 512, (tt + 1) * 512)
                            for dch in range(8):
                                bank = 4 + dcnt % 2
                                dcnt += 1
                                S.pe([lambda e, fi=fi, dch=dch, bank=bank, ts=ts: e.matmul(
                                    PS[:, bank, :], lhsT=wdall[:, dch, fi, :], rhs=aT[:, fi, ts], start=(fi == 0), stop=(fi == 10))
                                    for fi in range(11)],
                                    rd=[("wdall", dch)] + [("aT", fi, tt) for fi in range(11)], wr=[("ps", bank)])
                                S.op("dve", lambda e, bank=bank, dch=dch, ts=ts: e.scalar_tensor_tensor(
                                    out=xTt[:, dch, ts], in0=PS[:, bank, :], scalar=V(GV, dch), in1=xTt[:, dch, ts],
                                    op0=ALU.mult, op1=ALU.add),
                                    rd=[("ps", bank), ("vec", GV), ("xT", dch, tt)], wr=[("xT", dch, tt)])
                        Dtile(0)
                        st_fn(0)
                        Dtile(1)
                        ap_fn(0)
                        st_fn(1)
                        Dtile(2)
                        ap_fn(1)
                        st_fn(2)
                        Dtile(3)
                        ap_fn(2)
                        st_fn(3)
                        ap_fn(3)
                        continue
                    for dch in range(8):
                        if hook is not None:
                            hook()
                        sl = dcnt % 2
                        S.dma("pool", wdb[sl], wd_v[:, half * 11:(half + 1) * 11, dch * 128:(dch + 1) * 128],
                              wr=[("wdb", sl)], key=f"wdb{sl}")
                        for tt in range(4):
                            bank = 4 + (dcnt * 4 + tt) % 2
                            ts = slice(tt * 512, (tt + 1) * 512)
                            S.pe([lambda e, fi=fi, sl=sl, bank=bank, ts=ts: e.matmul(
                                PS[:, bank, :], lhsT=wdb[sl][:, fi, :], rhs=aT[:, fi, ts], start=(fi == 0), stop=(fi == 10))
                                for fi in range(11)],
                                rd=[("wdb", sl)] + [("aT", fi, tt) for fi in range(11)], wr=[("ps", bank)])
                            S.op("dve", lambda e, bank=bank, dch=dch, ts=ts: e.scalar_tensor_tensor(
                                out=xTt[:, dch, ts], in0=PS[:, bank, :], scalar=V(GV, dch), in1=xTt[:, dch, ts],
                                op0=ALU.mult, op1=ALU.add),
                                rd=[("ps", bank), ("vec", GV), ("xT", dch, tt)], wr=[("xT", dch, tt)])
                        dcnt += 1

            def ln_stats(xTt, tt, lnb):
                yb, ys, mean, msq, rstd, nmr, tmpv = lnb
                p = tt % 2
                ts = slice(tt * 512, (tt + 1) * 512)
                for dc in range(8):
                    s4 = dc % len(yb)
                    S.op("act", lambda e, dc=dc, s4=s4: e.activation(out=yb[s4], in_=xTt[:, dc, ts], func=AF.Copy),
                         rd=[("xT", dc, tt)], wr=[("yb", s4)])
                    S.op("dve", lambda e, dc=dc, s4=s4: e.tensor_tensor(out=ys[s4], in0=xTt[:, dc, ts], in1=xTt[:, dc, ts], op=ALU.mult),
                         rd=[("xT", dc, tt)], wr=[("ys", s4)])
                    S.pe([lambda e, dc=dc, s4=s4: e.matmul(PS[:, 6, :], lhsT=ones_b, rhs=yb[s4], start=(dc == 0), stop=(dc == 7)),
                          lambda e, dc=dc, s4=s4: e.matmul(PS[:, 7, :], lhsT=ones_b, rhs=ys[s4], start=(dc == 0), stop=(dc == 7))],
                         rd=["ones_b", ("yb", s4), ("ys", s4)], wr=[("ps", 6), ("ps", 7)])
                S.op("dve", lambda e: e.tensor_scalar(out=mean[p], in0=PS[:, 6, :], scalar1=1.0 / D, scalar2=None, op0=ALU.mult),
                     rd=[("ps", 6)], wr=[("mean", p)])
                S.op("dve", lambda e: e.tensor_tensor(out=msq[p], in0=mean[p], in1=mean[p], op=ALU.mult),
                     rd=[("mean", p)], wr=[("msq", p)])
                S.op("dve", lambda e: e.scalar_tensor_tensor(out=msq[p], in0=PS[:, 7, :], scalar=1.0 / D, in1=msq[p],
                                                             op0=ALU.mult, op1=ALU.subtract),
                     rd=[("ps", 7), ("msq", p)], wr=[("msq", p)])
                S.op("act", lambda e: e.activation(out=rstd[p], in_=msq[p], func=AF.Sqrt, bias=epsc[:, 0:1], scale=1.0),
                     rd=[("msq", p), "epsc"], wr=[("rstd", p)])
                S.op("dve", lambda e: e.reciprocal(out=rstd[p], in_=rstd[p]), rd=[("rstd", p)], wr=[("rstd", p)])
                S.op("dve", lambda e: e.scalar_tensor_tensor(out=nmr[p], in0=mean[p], scalar=-1.0, in1=rstd[p],
                                                             op0=ALU.mult, op1=ALU.mult),
                     rd=[("mean", p), ("rstd", p)], wr=[("nmr", p)])

            def ln_apply(xTt, tt, GA, GB, HA, HB, lnb):
                yb, ys, mean, msq, rstd, nmr, tmpv = lnb
                p = tt % 2
                ts = slice(tt * 512, (tt + 1) * 512)
                for dc in range(8):
                    tv = tmpv[dc % 2]
                    S.op("dve", lambda e, dc=dc, tv=tv: e.tensor_tensor(out=tv, in0=xTt[:, dc, ts], in1=rstd[p], op=ALU.mult),
                         rd=[("xT", dc, tt), ("rstd", p)], wr=[("tmpv", dc % 2)])
                    S.op("dve", lambda e, tv=tv: e.tensor_tensor(out=tv, in0=tv, in1=nmr[p], op=ALU.add),
                         rd=[("tmpv", dc % 2), ("nmr", p)], wr=[("tmpv", dc % 2)])
                    S.op("act", lambda e, dc=dc, tv=tv: e.activation(out=xTt[:, dc, ts], in_=tv, func=AF.Identity,
                                                                     scale=V(GA, dc), bias=V(GB, dc)),
                         rd=[("tmpv", dc % 2), ("vec", GA), ("vec", GB)], wr=[("xT", dc, tt)])
                    if HA is not None:
                        S.op("act", lambda e, dc=dc, tv=tv: e.activation(out=hT[:, dc, ts], in_=tv, func=AF.Identity,
                                                                         scale=V(HA, dc), bias=V(HB, dc)),
                             rd=[("tmpv", dc % 2), ("vec", HA), ("vec", HB)], wr=[("hT", dc, tt)])

            def ln_all(xTt, GA, GB, HA, HB, lnb, before=None, after=None):
                if before:
                    before(0)
                ln_stats(xTt, 0, lnb)
                for tt in range(4):
                    if tt + 1 < 4:
                        if before:
                            before(tt + 1)
                        ln_stats(xTt, tt + 1, lnb)
                    ln_apply(xTt, tt, GA, GB, HA, HB, lnb)
                    if after:
                        after(tt)

            def ln_bufs(st, nrot=4):
                def two(nm):
                    return [sb(f"{nm}{i}", [128, 512], F32, st) for i in range(2)]
                return ([sb(f"yb{i}", [128, 512], BF16, st) for i in range(nrot)],
                        [sb(f"ys{i}", [128, 512], BF16, st) for i in range(nrot)],
                        two("mean"), two("msq"), two("rstd"), two("nmr"), two("tmpv"))

            with contextlib.ExitStack() as F1:
                lnb = ln_bufs(F1, nrot=2)
                def after1(tt):
                    ts = slice(tt * 512, (tt + 1) * 512)
                    ibv = ibs[tt].ap().rearrange("a (two t) -> (a two) t", two=2).rearrange("(dc p) t -> p dc t", p=128)
                    S.dma("sp", ibv, hT[:, :, ts], rd=[("hT", dc, tt) for dc in range(8)], wr=[("ib", tt)], key=f"ib{tt}")
                    S.custom("pool", lambda e, tt=tt: e.collective_compute(
                        "AllGather", ALU.bypass, replica_groups=[[0, 1], [2, 3], [4, 5], [6, 7]],
                        ins=[ibs[tt].ap().opt()], outs=[obs[tt].ap().opt()]), f"cc{tt}", 1,
                        rd=[("ib", tt)], wr=[("ob", tt)])
                    S.dma("sp", x1sp[:, :, ts], xT[:, :, ts], rd=[("xT", dc, tt) for dc in range(8)], wr=[("x1sp", tt)],
                          key=f"x1sp{tt}")

                def ap1(tt):
                    ln_apply(xT, tt, AG1, AB1, A2, B2, lnb)
                    after1(tt)
                ffn(F1, f1g, f1u, f1d, G1H, xT, hook=ffn1_hook, ln_cb=(lambda tt: ln_stats(xT, tt, lnb), ap1))
                while hstate[0] < 30:
                    ffn1_hook()
            if DEBUG == "x1":
                S.dma("sp", dbg, xT, rd=[("xT", dc, tt) for dc in range(8) for tt in range(4)], wr=["dbg"], key="dbg")
        if DEBUG == "x1":
            S.barrier()
            return nc
        S.barrier(skip_prefix="c_cc", keep=[("ob", tt) for tt in range(4)])

        win_v = w_in.rearrange("(dc p) f -> p dc f", p=128)
        with contextlib.ExitStack() as P2:
            hprev = sb("hprev", [128, 8, NT], BF16, P2)
            for q in range(4):
                obv = obs[q].ap()[0:512, :].rearrange("a (two t) -> (a two) t", two=2).rearrange("(dc p) t -> p dc t", p=128)
                S.dma("sp", hprev[:, :, q * 512:(q + 1) * 512], obv, rd=[("ob", q)], wr=[("hprev", q)], key=f"hprev{q}")

            def hsrc(kt):
                return (hprev, "hprev") if kt < 4 else (hT, "hT")

            wrk_pre = sb("wrk_pre", [128, 8, 512], BF16, P2)
            wrv_pre = sb("wrv_pre", [128, 8, 512], BF16, P2)
            PAB = contextlib.ExitStack()
            Vaug = sb("Vaug", [128, 32, 8, 65], BF16, PAB)
            KTA0 = sb("KTA0", [128, 2 * NT], BF16, PAB)
            KTB0 = sb("KTB0", [128, 2 * NT], BF16, PAB)
            QTA0 = sb("QTA0", [128, NT], BF16, PAB)
            QTB0 = sb("QTB0", [128, NT], BF16, PAB)
            wqkv0 = sb("wqkv0", [128, 8, 256], BF16, PAB)
            p0bank = [0]

            def p0_q(qg):
                bank = 4 + p0bank[0] % 2
                p0bank[0] += 1
                ts = slice(qg * 512, (qg + 1) * 512)
                S.pe([lambda e, dc=dc: e.matmul(PS[:, bank, :], lhsT=wqkv0[:, dc, 0:128], rhs=hT[:, dc, ts],
                                                start=(dc == 0), stop=(dc == 7)) for dc in range(8)],
                     rd=["wqkv0"], wr=[("ps", bank)])
                S.op("dve", lambda e: e.tensor_scalar(out=QTA0[0:64, ts], in0=PS[0:64, bank, :], scalar1=0.125,
                                                      scalar2=None, op0=ALU.mult), rd=[("ps", bank)], wr=[("QTA0", qg)])
                S.op("dve", lambda e: e.tensor_scalar(out=QTB0[64:128, ts], in0=PS[64:128, bank, :], scalar1=0.125,
                                                      scalar2=None, op0=ALU.mult), rd=[("ps", bank)], wr=[("QTB0", qg)])

            def p0_k(kt):
                src, sk = hsrc(kt)
                bank = 4 + p0bank[0] % 2
                p0bank[0] += 1
                tsl = slice((kt % 4) * 512, (kt % 4 + 1) * 512)
                S.pe([lambda e, dc=dc: e.matmul(PS[:, bank, :], lhsT=wqkv0[:, dc, 128:256], rhs=src[:, dc, tsl],
                                                start=(dc == 0), stop=(dc == 7)) for dc in range(8)],
                     rd=["wqkv0"] + ([("hprev", kt)] if kt < 4 else []), wr=[("ps", bank)])
                S.op("act", lambda e: e.activation(out=KTA0[0:64, kt * 512:(kt + 1) * 512], in_=PS[0:64, bank, :], func=AF.Copy),
                     rd=[("ps", bank)], wr=[("KTA0", kt)])
                S.op("act", lambda e: e.activation(out=KTB0[64:128, kt * 512:(kt + 1) * 512], in_=PS[64:128, bank, :], func=AF.Copy),
                     rd=[("ps", bank)], wr=[("KTB0", kt)])
            with contextlib.ExitStack() as PA:
                wfl = sb("wfl", [128, 8, 8], BF16, PA)
                wv = sb("wv", [128, 8, 512], BF16, PA)
                S.dma("pool", wfl, win_v[:, :, 1536:1544], wr=["wfl"], key="wfl")
                S.dma("pool", wv, win_v[:, :, 1024:1536], wr=["wv"], key="wv")
                S.op("dve", lambda e: e.memset(Vaug[:, :, :, 64:65], 1.0), wr=["Vones"])
                S.dma("pool", wqkv0[:, :, 0:128], win_v[:, :, 0:128], wr=["wqkv0"], key="wqkv0")
                S.dma("pool", wqkv0[:, :, 128:256], win_v[:, :, 512:640], wr=["wqkv0"], key="wqkv0")
                S.op("dve", lambda e: e.memset(KTA0[64:67, :], 1.0), wr=["KTA0aug"])
                S.op("dve", lambda e: e.memset(KTB0[0:64, :], 1.0), wr=["KTB0aug"])
                S.op("dve", lambda e: e.memset(QTB0[0:64, :], 0.0), wr=["QTB0aug"])
                La = sb("La", [8, 2 * NT], F32, PA)
                Lb = sb("Lb", [8, 2 * NT], F32, PA)
                nfb = sb("nfb", [8, 1], F32, PA)
                offs = sb("offs", [8, 1], F32, PA)
                S.op("dve", lambda e: e.tensor_scalar(out=nfb, in0=fb, scalar1=-1.0, scalar2=None, op0=ALU.mult),
                     rd=["fb"], wr=["nfb"])
                for kt in range(4, 8):
                    src, sk = hsrc(kt)
                    tsl = slice((kt % 4) * 512, (kt % 4 + 1) * 512)
                    bank = kt % 2
                    S.pe([lambda e, dc=dc, src=src, tsl=tsl, bank=bank: e.matmul(
                        PS[0:8, bank, :], lhsT=wfl[:, dc, :], rhs=src[:, dc, tsl], start=(dc == 0), stop=(dc == 7))
                        for dc in range(8)],
                        rd=["wfl"] + ([(sk, kt % 4)] if sk == "hprev" else [("hT", dc, kt % 4) for dc in range(8)]),
                        wr=[("ps", bank)])
                    S.op("act", lambda e, kt=kt, bank=bank: e.activation(
                        out=La[:, kt * 512:(kt + 1) * 512], in_=PS[0:8, bank, :], func=AF.Exp, scale=-1.0, bias=nfb),
                        rd=[("ps", bank), "nfb"], wr=[("La", kt)])
                for kb in range(16, 32):
                    src, sk = hsrc(kb // 4)
                    t0 = (kb % 16) * 128
                    bank = 2 + kb % 2
                    S.pe([lambda e, dc=dc, src=src, t0=t0, bank=bank: e.matmul(
                        PS[:, bank, :], lhsT=src[:, dc, t0:t0 + 128], rhs=wv[:, dc, :], start=(dc == 0), stop=(dc == 7))
                        for dc in range(8)],
                        rd=["wv"] + ([("hprev", kb // 4)] if kb < 16 else []), wr=[("ps", bank)])
                    S.op("act", lambda e, kb=kb, bank=bank: e.activation(
                        out=Vaug[:, kb, :, 0:64], in_=PS[:, bank, :].rearrange("p (h d) -> p h d", d=64), func=AF.Copy),
                        rd=[("ps", bank)], wr=[("V", kb)])
                for qg in range(4):
                    p0_q(qg)
                for kt in range(4, 8):
                    p0_k(kt)
                for kt in range(0, 4):
                    src, sk = hsrc(kt)
                    tsl = slice((kt % 4) * 512, (kt % 4 + 1) * 512)
                    bank = kt % 2
                    S.pe([lambda e, dc=dc, src=src, tsl=tsl, bank=bank: e.matmul(
                        PS[0:8, bank, :], lhsT=wfl[:, dc, :], rhs=src[:, dc, tsl], start=(dc == 0), stop=(dc == 7))
                        for dc in range(8)],
                        rd=["wfl"] + ([(sk, kt % 4)] if sk == "hprev" else [("hT", dc, kt % 4) for dc in range(8)]),
                        wr=[("ps", bank)])
                    S.op("act", lambda e, kt=kt, bank=bank: e.activation(
                        out=La[:, kt * 512:(kt + 1) * 512], in_=PS[0:8, bank, :], func=AF.Exp, scale=-1.0, bias=nfb),
                        rd=[("ps", bank), "nfb"], wr=[("La", kt)])
                S.op("act", lambda e: e.activation(out=Lb, in_=La, func=AF.Ln, scale=1.0, bias=1.0),
                     rd=[("La", kt) for kt in range(8)], wr=["Lb"])
                for kb in range(0, 16):
                    src, sk = hsrc(kb // 4)
                    t0 = (kb % 16) * 128
                    bank = 2 + kb % 2
                    S.pe([lambda e, dc=dc, src=src, t0=t0, bank=bank: e.matmul(
                        PS[:, bank, :], lhsT=src[:, dc, t0:t0 + 128], rhs=wv[:, dc, :], start=(dc == 0), stop=(dc == 7))
                        for dc in range(8)],
                        rd=["wv"] + ([("hprev", kb // 4)] if kb < 16 else []), wr=[("ps", bank)])
                    S.op("act", lambda e, kb=kb, bank=bank: e.activation(
                        out=Vaug[:, kb, :, 0:64], in_=PS[:, bank, :].rearrange("p (h d) -> p h d", d=64), func=AF.Copy),
                        rd=[("ps", bank)], wr=[("V", kb)])
                for kt in range(0, 4):
                    p0_k(kt)
                ones8t = sb("ones8t", [8, NT], F32, PA)
                S.op("dve", lambda e: e.memset(ones8t, 1.0), wr=["ones8"])
                S.op("dve", lambda e: e.tensor_tensor_scan(out=La[:, 0:NT], data0=ones8t, data1=Lb[:, 0:NT], initial=0.0,
                                                           op0=ALU.mult, op1=ALU.add),
                     rd=["Lb", "ones8"] + [("La", kt) for kt in range(8)], wr=["Cp"])
                S.op("dve", lambda e: e.tensor_tensor(out=offs, in0=La[:, NT - 1:NT], in1=gs[0:8, 0:1], op=ALU.mult),
                     rd=["Cp", "gs"], wr=["offs"])
                S.op("dve", lambda e: e.tensor_tensor_scan(out=La[:, NT:2 * NT], data0=ones8t, data1=Lb[:, NT:2 * NT],
                                                           initial=offs, op0=ALU.mult, op1=ALU.add),
                     rd=["Lb", "ones8", "offs", "Cp"], wr=["Co"])
                O3w = sb("O3w", [128, 3, 256], BF16, PA)
                S.op("dve", lambda e: e.memset(O3w, 1.0), wr=["O3w"])
                for hh in range(8):
                    S.dma("sp", qaug_d[hh, 3:6, :].rearrange("z (b f) -> b z f", f=256), O3w[0:8, :, :],
                          rd=["O3w"], wr=[("qaug1", hh)], key=f"kaug0_{hh}")
                C128 = sb("C128", [128, 256], F32, PA)
                r1w = sb("r1w", [128, 256], F32, PA)
                hfw = sb("hfw", [128, 256], F32, PA)
                H3q = sb("H3q", [128, 3, 256], BF16, PA)
                H3k = sb("H3k", [128, 3, 256], BF16, PA)
                S.dma("sp", cdram, La, rd=["Cp", "Co"], wr=["cdram"], key="cdram")
                S.dma("sp", C128, cdram.rearrange("h (b f) -> (h b) f", f=256), rd=["cdram"], wr=["C128"], key="c128")

                def split3w(first_op, H3x, hk):
                    S.op("dve", first_op, rd=["C128", "gs"], wr=["r1w"])
                    for z in range(3):
                        S.op("dve", lambda e, z=z: e.tensor_copy(out=H3x[:, z, :], in_=r1w), rd=["r1w"], wr=[(hk, z)])
                        if z < 2:
                            S.op("dve", lambda e, z=z: e.tensor_copy(out=hfw, in_=H3x[:, z, :]), rd=[(hk, z)], wr=["hfw"])
                            S.op("dve", lambda e: e.tensor_tensor(out=r1w, in0=r1w, in1=hfw, op=ALU.subtract),
                                 rd=["r1w", "hfw"], wr=["r1w"])

                split3w(lambda e: e.tensor_scalar(out=r1w, in0=C128, scalar1=-1.0, scalar2=None, op0=ALU.mult), H3q, "H3q")
                split3w(lambda e: e.tensor_scalar(out=r1w, in0=C128, scalar1=gs[:, 2:3], scalar2=None, op0=ALU.add), H3k, "H3k")
                for hh in range(8):
                    S.dma("sp", qaug_d[hh, 0:3, :].rearrange("z (b f) -> b z f", f=256), H3q[hh * 16 + 8:hh * 16 + 16, :, :],
                          rd=[("H3q", z) for z in range(3)], wr=[("qaug0", hh)], key=f"qaug0_{hh}")
                    S.dma("sp", kaug_d[hh, :, :].rearrange("z (b f) -> b z f", f=256), H3k[hh * 16:hh * 16 + 16, :, :],
                          rd=[("H3k", z) for z in range(3)], wr=[("kaug0", hh)], key=f"kaug0_{hh}")
            S.barrier()

            with contextlib.ExitStack() as PB:
                KTA = [KTA0, sb("KTA1", [128, 2 * NT], BF16, PB)]
                KTB = [KTB0, sb("KTB1", [128, 2 * NT], BF16, PB)]
                QTA = [QTA0, sb("QTA1", [128, NT], BF16, PB)]
                QTB = [QTB0, sb("QTB1", [128, NT], BF16, PB)]
                wqkv = [wqkv0, sb("wqkv1", [128, 8, 256], BF16, PB)]
                PT = [sb(f"PT{i}", [128, 2, 512], BF16, PB) for i in range(3)]
                otm = [sb("otm0", [128, 16, 128], BF16, PB)] * 2
                rden = [sb(f"rden{i}", [128, 4], F32, PB) for i in range(2)]
                oTs = [sb(f"oTs{i}", [128, 512], F32, PB) for i in range(2)]
                ost = [sb(f"ost{i}", [128, 4, 128], BF16, PB) for i in range(2)]
                for i in range(1, 2):
                    S.op("dve", lambda e, i=i: e.memset(KTA[i][64:67, :], 1.0), wr=[("KTAaug", i)])
                    S.op("dve", lambda e, i=i: e.memset(KTB[i][0:64, :], 1.0), wr=[("KTBaug", i)])
                    S.op("dve", lambda e, i=i: e.memset(QTB[i][0:64, :], 0.0), wr=[("QTBaug", i)])
                tbank = [0]

                def proj_tasks(hp):
                    s = hp % 2
                    tasks = []

                    def t_load():
                        S.dma("pool", wqkv[s][:, :, 0:128], win_v[:, :, hp * 128:(hp + 1) * 128], wr=[("wqkv", s)], key=f"wqkv{s}")
                        S.dma("pool", wqkv[s][:, :, 128:256], win_v[:, :, 512 + hp * 128:512 + (hp + 1) * 128], wr=[("wqkv", s)],
                              key=f"wqkv{s}")
                        S.dma("sp", QTA[s][64:70, :], qaug_d[2 * hp], wr=[("QTAaug", s)], key=f"qtaA{s}")
                        S.dma("sp", QTB[s][58:64, :], qaug_d[2 * hp + 1], wr=[("QTBaug", s)], key=f"qtaB{s}")
                        S.dma("sp", KTA[s][67:70, :], kaug_d[2 * hp], wr=[("KTAaug", s)], key=f"ktaA{s}")
                        S.dma("sp", KTB[s][61:64, :], kaug_d[2 * hp + 1], wr=[("KTBaug", s)], key=f"ktaB{s}")
                    tasks.append(t_load)
                    for qg in range(4):
                        def t_q(qg=qg):
                            bank = tbank[0] % 2
                            tbank[0] += 1
                            ts = slice(qg * 512, (qg + 1) * 512)
                            S.pe([lambda e, dc=dc: e.matmul(PS[:, bank, :], lhsT=wqkv[s][:, dc, 0:128], rhs=hT[:, dc, ts],
                                                            start=(dc == 0), stop=(dc == 7)) for dc in range(8)],
                                 rd=[("wqkv", s)], wr=[("ps", bank)])
                            S.op("dve", lambda e: e.tensor_scalar(out=QTA[s][0:64, ts], in0=PS[0:64, bank, :], scalar1=0.125,
                                                                  scalar2=None, op0=ALU.mult),
                                 rd=[("ps", bank)], wr=[("QTA", s, qg)])
                            S.op("dve", lambda e: e.tensor_scalar(out=QTB[s][64:128, ts], in0=PS[64:128, bank, :], scalar1=0.125,
                                                                  scalar2=None, op0=ALU.mult),
                                 rd=[("ps", bank)], wr=[("QTB", s, qg)])
                        tasks.append(t_q)
                    for kt in range(8):
                        def t_k(kt=kt):
                            src, sk = hsrc(kt)
                            bank = tbank[0] % 2
                            tbank[0] += 1
                            tsl = slice((kt % 4) * 512, (kt % 4 + 1) * 512)
                            S.pe([lambda e, dc=dc: e.matmul(PS[:, bank, :], lhsT=wqkv[s][:, dc, 128:256], rhs=src[:, dc, tsl],
                                                            start=(dc == 0), stop=(dc == 7)) for dc in range(8)],
                                 rd=[("wqkv", s)], wr=[("ps", bank)])
                            S.op("dve", lambda e: e.tensor_copy(out=KTA[s][0:64, kt * 512:(kt + 1) * 512], in_=PS[0:64, bank, :]),
                                 rd=[("ps", bank)], wr=[("KTA", s, kt)])
                            S.op("dve", lambda e: e.tensor_copy(out=KTB[s][64:128, kt * 512:(kt + 1) * 512], in_=PS[64:128, bank, :]),
                                 rd=[("ps", bank)], wr=[("KTB", s, kt)])
                        tasks.append(t_k)
                    return tasks

                S.dma("pool", wrk_pre, win_v[:, :, 2056:2568], wr=[("wr", 1)], key="wr1")
                S.dma("pool", wrv_pre, win_v[:, :, 2568:3080], wr=[("wr", 2)], key="wr2")
                S.dma("sp", QTA[0][64:70, :], qaug_d[0], wr=[("QTAaug", 0)], key="qtaA0")
                S.dma("sp", QTB[0][58:64, :], qaug_d[1], wr=[("QTBaug", 0)], key="qtaB0")
                S.dma("sp", KTA[0][67:70, :], kaug_d[0], wr=[("KTAaug", 0)], key="ktaA0")
                S.dma("sp", KTB[0][61:64, :], kaug_d[1], wr=[("KTBaug", 0)], key="ktaB0")
                gstep = 0
                qcnt = 0
                for hp in range(4):
                    s = hp % 2
                    pending = proj_tasks(hp + 1) if hp + 1 < 4 else []
                    ucount = 0
                    for hl in range(2):
                        h = 2 * hp + hl
                        KTx, QTx = (KTA[s], QTA[s]) if hl == 0 else (KTB[s], QTB[s])
                        r0, r1 = (0, 70) if hl == 0 else (0, 128)
                        kname, qname = ("KTA", "QTA") if hl == 0 else ("KTB", "QTB")
                        units = []
                        for qg in range(4):
                            nfull = 16 + 4 * qg
                            for kb in range(0, nfull, 2):
                                units.append((qg, kb, 2, 0))
                            for m in range(4):
                                units.append((qg, nfull + m, 1, m * 128))
                        deferred = []

                        def emit_S(i, KTx=KTx, QTx=QTx, r0=r0, r1=r1, kname=kname, qname=qname, units=units):
                            qg, kb, nb, c0 = units[i]
                            b0 = 2 + 2 * ((gstep + i) % 2)
                            fns = []
                            for z in range(nb):
                                fns.append(lambda e, z=z: e.matmul(
                                    PS[:, b0 + z, c0:512], lhsT=KTx[r0:r1, (kb + z) * 128:(kb + z + 1) * 128],
                                    rhs=QTx[r0:r1, qg * 512 + c0:(qg + 1) * 512], start=True, stop=(nb == 2)))
                            if nb == 1:
                                fns.append(lambda e: e.matmul(PS[:, b0, c0:c0 + 128], lhsT=ident_b, rhs=mask_b,
                                                              start=False, stop=True))
                            S.pe(fns, rd=[(kname, s, kb // 4), (kname + "aug", s), (qname, s, qg), (qname + "aug", s), "ident_b", "mask_b"],
                                 wr=[("ps", b0), ("ps", b0 + 1)] if nb == 2 else [("ps", b0)])

                        def emit_E(i, units=units):
                            qg, kb, nb, c0 = units[i]
                            b0 = 2 + 2 * ((gstep + i) % 2)
                            pi = (gstep + i) % 3
                            if nb == 2:
                                S.op("act", lambda e: e.activation(out=PT[pi], in_=PS[:, b0:b0 + 2, :], func=AF.Exp),
                                     rd=[("ps", b0), ("ps", b0 + 1)], wr=[("PT", pi)])
                            else:
                                S.op("act", lambda e: e.activation(out=PT[pi][:, 0, c0:512], in_=PS[:, b0, c0:512], func=AF.Exp),
                                     rd=[("ps", b0)], wr=[("PT", pi)])

                        def emit_PV(i, h=h, hl=hl, units=units, deferred=deferred):
                            nonlocal qcnt
                            qg, kb, nb, c0 = units[i]
                            nk = 16 + 4 * qg + 4
                            pi = (gstep + i) % 3
                            obk = 6 + qg % 2
                            S.pe([lambda e, z=z: e.matmul(PS[0:65, obk, c0:512], lhsT=Vaug[:, kb + z, h, :], rhs=PT[pi][:, z, c0:512],
                                                          start=(kb + z == 0), stop=(kb + z == nk - 1)) for z in range(nb)],
                                 rd=[("PT", pi)], wr=[("ps", obk)])
                            if kb + nb == nk:
                                q2 = qcnt % 2
                                qcnt += 1
                                S.op("dve", lambda e: e.tensor_copy(out=oTs[q2][0:65, :], in_=PS[0:65, obk, :]),
                                     rd=[("ps", obk)], wr=[("oTs", q2)])

                                def fin(q2=q2, qg=qg, hl=hl):
                                    tb = tbank[0] % 2
                                    tbank[0] += 1
                                    tv = PS[:, tb, 0:260].rearrange("p (n d) -> p n d", d=65)
                                    S.pe([lambda e, n=n: e.transpose(out=tv[:, n, :], in_=oTs[q2][0:65, n * 128:(n + 1) * 128],
                                                                     identity=ident_f[0:65, 0:65]) for n in range(4)],
                                         rd=[("oTs", q2), "ident_f"], wr=[("ps", tb)])
                                    S.op("dve", lambda e: e.reciprocal(out=rden[q2], in_=tv[:, :, 64]), rd=[("ps", tb)],
                                         wr=[("rden", q2)])
                                    S.op("dve", lambda e: e.tensor_tensor(
                                        out=otm[s][:, qg * 4:(qg + 1) * 4, hl * 64:(hl + 1) * 64], in0=tv[:, :, 0:64],
                                        in1=rden[q2].unsqueeze(2).broadcast_to([128, 4, 64]), op=ALU.mult),
                                        rd=[("ps", tb), ("rden", q2)], wr=[("otm", 0, qg, hl)])
                                deferred.append([i + 3, fin])

                        n_steps = len(units)
                        emit_S(0)
                        for i in range(n_steps):
                            for dfr in list(deferred):
                                if dfr[0] <= i:
                                    dfr[1]()
                                    deferred.remove(dfr)
                            if pending and ucount % 11 == 3:
                                pending.pop(0)()
                            ucount += 1
                            if i + 1 < n_steps:
                                emit_S(i + 1)
                            emit_E(i)
                            emit_PV(i)
                        for dfr in list(deferred):
                            dfr[1]()
                        gstep += n_steps
                    while pending:
                        pending.pop(0)()
                    for b4 in range(4):
                        tb = tbank[0] % 2
                        tbank[0] += 1
                        pvb = PS[:, tb, :].bitcast(BF16)[:, 0:512].rearrange("p (c t) -> p c t", t=128)
                        S.pe([lambda e, c=c, b4=b4, pvb=pvb: e.transpose(out=pvb[:, c, :], in_=otm[s][:, b4 * 4 + c, :],
                                                                        identity=ident_b) for c in range(4)],
                             rd=[("otm", 0, b4, hl) for hl in range(2)] + ["ident_b"], wr=[("ps", tb)])
                        oq = (hp * 4 + b4) % 2
                        S.op("dve", lambda e, oq=oq, pvb=pvb: e.tensor_copy(out=ost[oq], in_=pvb),
                             rd=[("ps", tb)], wr=[("ost", oq)])
                        S.dma("sp", oT_d[:, hp, b4 * 512:(b4 + 1) * 512].rearrange("p (c t) -> p c t", t=128), ost[oq],
                              rd=[("ost", oq)], wr=[("oTd", hp, b4)], key=f"qaug0_{oq}")
            S.barrier()
            PAB.close()

            with contextlib.ExitStack() as PC:
                wr_ = [sb("wr0", [128, 8, 512], BF16, PC), wrk_pre, wrv_pre, sb("wr3", [128, 8, 512], BF16, PC)]
                ropeP = sb("ropeP", [128, 16, 128], F32, PC)
                ropeO = sb("ropeO", [128, 16, 128], F32, PC)
                decT = sb("decT_s", [128, 4, 128], F32, PC)
                Ctab = sb("Ctab", [128, 4, 128], F32, PC)
                dmy = sb("dmy", [128, 4], F32, PC)
                S.op("dve", lambda e: e.memset(dmy, 1.0), wr=["dmy0", "dmy1"])
                St = sb("St", [128, 4, 128], F32, PC)
                Stmp = sb("Stmp", [128, 4, 128], F32, PC)

                def two(name, shape, dt, n=2):
                    return [sb(f"{name}{i}", shape, dt, PC) for i in range(n)]

                Sbf = two("Sbf", [128, 4, 128], BF16, 3)
                qkrb = two("qkrb", [128, 2, 4, 128], BF16)
                qrb = [qkrb[0][:, 0], qkrb[1][:, 0]]
                krb = [qkrb[0][:, 1], qkrb[1][:, 1]]
                t1w = two("t1w", [128, 2, 4, 64], F32)
                t2w = two("t2w", [128, 2, 4, 64], F32)
                kd = two("kd", [128, 4, 128], BF16)
                vb = two("vb", [128, 4, 128], BF16, 3)
                t1 = two("t1", [128, 4, 64], F32)
                t2 = two("t2", [128, 4, 64], F32)
                qkT = two("qkT", [128, 2, 4, 128], BF16, 3)
                STb = two("STb", [128, 4, 128], BF16)
                intra = two("intra", [128, 4, 128], F32)
                tmpc = two("tmpc", [128, 4, 128], F32)
                osb = two("osb", [128, 4, 128], F32)
                ynb = two("ynb", [128, 4, 128], BF16)
                stats = two("stats", [128, 4, 6], F32)
                mv = two("mv", [128, 4, 2], F32)
                rs = two("rs", [128, 4], F32)
                sg = two("sg", [128, 4, 128], F32, 4)
                tg = two("tg", [128, 4, 128], F32)
                ostr = two("ostr", [128, 4, 128], BF16)
                for i, c0 in ((0, 1544), (3, 3080)):
                    S.dma("pool", wr_[i], win_v[:, :, c0:c0 + 512], wr=[("wr", i)], key=f"wr{i}")
                S.dma("sp", ropeP, rope_prev, wr=["ropeP"], key="rope_1")
                S.dma("sp", ropeO, rope_own, wr=["ropeO"], key="rope_2")
                S.dma("sp", decT, decT_in, wr=["decT"], key="rope_3")
                for hh in range(4):
                    S.op("dve", lambda e, hh=hh: e.memset(Ctab[:, hh, :], GAMMA[hh] ** 128), wr=["Ctab"])

                def bc(ap2, n=128):
                    return ap2.unsqueeze(2).broadcast_to([128, 4, n])

                def rotary(bank, dst, dkey, rope, rkey, j, p):
                    X = PS[:, bank, :].rearrange("p (h z d) -> p h z d", z=2, d=64)
                    a = X[:, :, 0, :]
                    b = X[:, :, 1, :]
                    cs = rope[:, j, 0:64].unsqueeze(1).broadcast_to([128, 4, 64])
                    sn = rope[:, j, 64:128].unsqueeze(1).broadcast_to([128, 4, 64])
                    T1, T2 = t1[p], t2[p]
                    k1, k2 = ("t1", p), ("t2", p)
                    S.op("dve", lambda e: e.tensor_tensor(out=T1, in0=a, in1=cs, op=ALU.mult), rd=[("ps", bank), rkey], wr=[k1])
                    S.op("dve", lambda e: e.tensor_tensor(out=T2, in0=b, in1=sn, op=ALU.mult), rd=[("ps", bank), rkey], wr=[k2])
                    S.op("dve", lambda e: e.tensor_tensor(out=dst[:, :, 0:64], in0=T1, in1=T2, op=ALU.subtract),
                         rd=[k1, k2], wr=[(dkey, 0)])
                    S.op("dve", lambda e: e.tensor_tensor(out=T1, in0=a, in1=sn, op=ALU.mult), rd=[("ps", bank), rkey], wr=[k1])
                    S.op("dve", lambda e: e.tensor_tensor(out=T2, in0=b, in1=cs, op=ALU.mult), rd=[("ps", bank), rkey], wr=[k2])
                    S.op("dve", lambda e: e.tensor_tensor(out=dst[:, :, 64:128], in0=T1, in1=T2, op=ALU.add),
                         rd=[k1, k2], wr=[(dkey, 1)])

                def proj(bank, src, tsl, wi):
                    S.pe([lambda e, dc=dc: e.matmul(PS[:, bank, :], lhsT=src[:, dc, tsl], rhs=wr_[wi][:, dc, :],
                                                    start=(dc == 0), stop=(dc == 7)) for dc in range(8)],
                         rd=[("wr", wi)], wr=[("ps", bank)])

                def s1_proj(j):
                    p = j % 2
                    bk, bv = (0, 1) if p == 0 else (2, 3)
                    tsl = slice(j * 128, (j + 1) * 128)
                    proj(bk, hprev, tsl, 1)
                    proj(bv, hprev, tsl, 2)

                def s1_rest(j):
                    p = j % 2
                    bk, bv = (0, 1) if p == 0 else (2, 3)
                    rotary(bk, krb[p], ("krb", p), ropeP, "ropeP", j, p)
                    S.op("act", lambda e: e.activation(
                        out=vb[p], in_=PS[:, bv, :].rearrange("p (h d) -> p h d", d=128), func=AF.Copy),
                        rd=[("ps", bv)], wr=[("vb", p)])
                    S.op("pool", lambda e: e.tensor_tensor(
                        out=kd[p], in0=krb[p], in1=bc(qkdec[:, 8 + j * 4:12 + j * 4]), op=ALU.mult),
                        rd=[(("krb", p), 0), (("krb", p), 1), "qkdec"], wr=[("kd", p)])
                    S.pe([lambda e, hh=hh: e.matmul(
                        PS[:, 4, hh * 128:(hh + 1) * 128], lhsT=kd[p][:, hh, :], rhs=vb[p][:, hh, :],
                        start=(j == 0 and hh == 0), stop=(j == 15), skip_group_check=True) for hh in range(4)],
                        rd=[("kd", p), ("vb", p)], wr=[("ps", 4)])

                s1_proj(0)
                for j in range(16):
                    if j + 1 < 16:
                        s1_proj(j + 1)
                    s1_rest(j)
                S.op("dve", lambda e: e.tensor_scalar(out=St, in0=PS[:, 4, :].rearrange("p (h d) -> p h d", d=128),
                                                      scalar1=gs[:, 0:1], scalar2=None, op0=ALU.mult),
                     rd=[("ps", 4), "gs"], wr=["St"])

                def stageA(j):
                    p = j % 2
                    h3 = j % 3
                    tsl = slice(j * 128, (j + 1) * 128)
                    proj(0, hT, tsl, 0)
                    proj(1, hT, tsl, 1)
                    proj(2, hT, tsl, 2)
                    S.pe([lambda e, dc=dc, hh=hh: e.matmul(PS[:, 3, hh * 128:(hh + 1) * 128],
                                                           lhsT=wr_[3][:, dc, hh * 128:(hh + 1) * 128],
                                                           rhs=hT[:, dc, tsl], start=(dc == 0), stop=(dc == 7))
                          for hh in range(4) for dc in range(8)], rd=[("wr", 3)], wr=[("ps", 3)])
                    X2 = PS[:, 0:2, :].rearrange("p b (h z d) -> p b h z d", z=2, d=64)
                    a2 = X2[:, :, :, 0, :]
                    b2 = X2[:, :, :, 1, :]
                    cs2 = ropeO[:, j, 0:64].unsqueeze(1).unsqueeze(1).broadcast_to([128, 2, 4, 64])
                    sn2 = ropeO[:, j, 64:128].unsqueeze(1).unsqueeze(1).broadcast_to([128, 2, 4, 64])
                    T1, T2 = t1w[p], t2w[p]
                    kq = [(("qrb", p), 0), (("krb", p), 0)]
                    kq1 = [(("qrb", p), 1), (("krb", p), 1)]
                    pk = [("ps", 0), ("ps", 1), "ropeO"]
                    S.op("dve", lambda e: e.tensor_tensor(out=T1, in0=a2, in1=cs2, op=ALU.mult), rd=pk, wr=[("t1w", p)])
                    S.op("dve", lambda e: e.tensor_tensor(out=T2, in0=b2, in1=sn2, op=ALU.mult), rd=pk, wr=[("t2w", p)])
                    S.op("dve", lambda e: e.tensor_tensor(out=qkrb[p][:, :, :, 0:64], in0=T1, in1=T2, op=ALU.subtract),
                         rd=[("t1w", p), ("t2w", p)], wr=kq)
                    S.op("dve", lambda e: e.tensor_tensor(out=T1, in0=a2, in1=sn2, op=ALU.mult), rd=pk, wr=[("t1w", p)])
                    S.op("dve", lambda e: e.tensor_tensor(out=T2, in0=b2, in1=cs2, op=ALU.mult), rd=pk, wr=[("t2w", p)])
                    S.op("dve", lambda e: e.tensor_tensor(out=qkrb[p][:, :, :, 64:128], in0=T1, in1=T2, op=ALU.add),
                         rd=[("t1w", p), ("t2w", p)], wr=kq1)
                    S.op("act", lambda e: e.activation(out=vb[h3], in_=PS[:, 2, :].rearrange("p (h d) -> p h d", d=128), func=AF.Copy),
                         rd=[("ps", 2)], wr=[("vb", h3)])
                    S.op("act", lambda e: e.activation(out=sg[j % 4], in_=PS[:, 3, :].rearrange("p (h d) -> p h d", d=128), func=AF.Silu),
                         rd=[("ps", 3)], wr=[("sg", j % 4)])
                    S.op("act", lambda e: e.activation(out=dmy[:, 2:3], in_=dmy[:, 3:4], func=AF.Sqrt), wr=["dmy1"])
                    S.op("pool", lambda e: e.tensor_tensor(out=kd[p], in0=krb[p], in1=bc(qkdec[:, 4:8]), op=ALU.mult),
                         rd=[(("krb", p), 0), (("krb", p), 1), "qkdec"], wr=[("kd", p)])

                def stageA2(j):
                    p = j % 2
                    h3 = j % 3
                    p4 = PS[:, 4, :].bitcast(BF16)
                    S.pe([lambda e, hh=hh: e.transpose(out=p4[:, hh * 128:(hh + 1) * 128], in_=qrb[p][:, hh, :], identity=ident_b)
                          for hh in range(4)] +
                         [lambda e, hh=hh: e.transpose(out=p4[:, 512 + hh * 128:512 + (hh + 1) * 128], in_=krb[p][:, hh, :],
                                                       identity=ident_b) for hh in range(4)],
                         rd=[(("qrb", p), 0), (("qrb", p), 1), (("krb", p), 0), (("krb", p), 1), "ident_b"], wr=[("ps", 4)])
                    S.op("act", lambda e: e.activation(out=qkT[h3], in_=p4.rearrange("p (z h d) -> p z h d", z=2, d=128), func=AF.Copy),
                         rd=[("ps", 4)], wr=[("qkT", h3)])
                    S.op("act", lambda e: e.activation(out=Sbf[h3], in_=St, func=AF.Copy), rd=["St"], wr=[("Sbf", h3)])
                    S.pe([lambda e, hh=hh: e.matmul(PS[:, 3, hh * 128:(hh + 1) * 128], lhsT=kd[p][:, hh, :], rhs=vb[h3][:, hh, :],
                                                    start=True, stop=True) for hh in range(4)],
                         rd=[("kd", p), ("vb", h3)], wr=[("ps", 3)])
                    S.op("pool", lambda e: e.tensor_tensor(out=Stmp, in0=St, in1=Ctab, op=ALU.mult),
                         rd=["St", "Ctab"], wr=["Stmp"])
                    S.op("dve", lambda e: e.tensor_tensor(out=St, in0=Stmp, in1=PS[:, 3, :].rearrange("p (h d) -> p h d", d=128),
                                                          op=ALU.add), rd=["Stmp", ("ps", 3)], wr=["St"])

                def stageB(j):
                    p = j % 2
                    h3 = j % 3
                    qT = qkT[h3][:, 0]
                    kT = qkT[h3][:, 1]
                    S.pe([lambda e, hh=hh: e.matmul(PS[:, 5, hh * 128:(hh + 1) * 128], lhsT=kT[:, hh, :], rhs=qT[:, hh, :],
                                                    start=True, stop=True) for hh in range(4)],
                         rd=[("qkT", h3)], wr=[("ps", 5)])
                    S.op("dve", lambda e: e.tensor_tensor(out=STb[p], in0=PS[:, 5, :].rearrange("p (h d) -> p h d", d=128), in1=decT,
                                                          op=ALU.mult), rd=[("ps", 5), "decT"], wr=[("STb", p)])
                    S.pe([lambda e, hh=hh: e.matmul(PS[:, 6, hh * 128:(hh + 1) * 128], lhsT=STb[p][:, hh, :], rhs=vb[h3][:, hh, :],
                                                    start=True, stop=True) for hh in range(4)],
                         rd=[("STb", p), ("vb", h3)], wr=[("ps", 6)])
                    S.pe([lambda e, hh=hh: e.matmul(PS[:, 5, hh * 128:(hh + 1) * 128], lhsT=qT[:, hh, :], rhs=Sbf[h3][:, hh, :],
                                                    start=True, stop=True) for hh in range(4)],
                         rd=[("qkT", h3), ("Sbf", h3)], wr=[("ps", 5)])
                    S.op("act", lambda e: e.activation(out=intra[p], in_=PS[:, 6, :].rearrange("p (h d) -> p h d", d=128), func=AF.Copy),
                         rd=[("ps", 6)], wr=[("intra", p)])
                    S.op("dve", lambda e: e.tensor_tensor(out=tmpc[p], in0=PS[:, 5, :].rearrange("p (h d) -> p h d", d=128),
                                                          in1=bc(qkdec[:, 0:4]), op=ALU.mult),
                         rd=[("ps", 5), "qkdec"], wr=[("tmpc", p)])
                    S.op("dve", lambda e: e.tensor_tensor(out=osb[p], in0=tmpc[p], in1=intra[p], op=ALU.add),
                         rd=[("tmpc", p), ("intra", p)], wr=[("osb", p)])
                    for hh in range(4):
                        S.op("dve", lambda e, hh=hh: e.bn_stats(out=stats[p][:, hh, :], in_=osb[p][:, hh, :]),
                             rd=[("osb", p)], wr=[("stats", p, hh)])
                        S.op("dve", lambda e, hh=hh: e.bn_aggr(out=mv[p][:, hh, :], in_=stats[p][:, hh, :]),
                             rd=[("stats", p, hh)], wr=[("mv", p, hh)])
                    S.op("act", lambda e: e.activation(out=rs[p], in_=mv[p][:, :, 1], func=AF.Sqrt, bias=epsc[:, 0:1], scale=1.0),
                         rd=[("mv", p, hh) for hh in range(4)] + ["epsc"], wr=[("rs", p)])
                    S.op("act", lambda e: e.activation(out=dmy[:, 0:1], in_=dmy[:, 1:2], func=AF.Silu), wr=["dmy0"])
                    S.op("dve", lambda e: e.reciprocal(out=rs[p], in_=rs[p]), rd=[("rs", p)], wr=[("rs", p)])
                    S.op("dve", lambda e: e.tensor_tensor(out=osb[p], in0=osb[p], in1=bc(mv[p][:, :, 0]), op=ALU.subtract),
                         rd=[("osb", p)] + [("mv", p, hh) for hh in range(4)], wr=[("osb", p)])
                    S.op("dve", lambda e: e.tensor_tensor(out=ynb[p], in0=osb[p], in1=bc(rs[p]), op=ALU.mult),
                         rd=[("osb", p), ("rs", p)], wr=[("ynb", p)])

                def stageC(j):
                    p = j % 2
                    h3 = j % 3
                    tsl = slice(j * 128, (j + 1) * 128)
                    p7 = PS[:, 7, :].bitcast(BF16)
                    S.pe([lambda e, hh=hh: e.transpose(out=p7[:, hh * 128:(hh + 1) * 128], in_=ynb[p][:, hh, :], identity=ident_b)
                          for hh in range(4)], rd=[("ynb", p), "ident_b"], wr=[("ps", 7)])
                    for hh in range(4):
                        S.op("act", lambda e, hh=hh: e.activation(out=tg[p][:, hh, :], in_=p7[:, hh * 128:(hh + 1) * 128],
                                                                  func=AF.Identity, scale=gnv[:, hh:hh + 1], bias=gnv[:, 4 + hh:5 + hh]),
                             rd=[("ps", 7), "gnv"], wr=[("tg", p, hh)])
                    S.op("dve", lambda e: e.tensor_tensor(out=ostr[p], in0=tg[p], in1=sg[j % 4], op=ALU.mult),
                         rd=[("tg", p, hh) for hh in range(4)] + [("sg", j % 4)], wr=[("ostr", p)])
                    S.dma("sp", oT_d[:, 4:8, tsl], ostr[p], rd=[("ostr", p)], wr=[("oTd_r", j)], key=f"qaug0_{2 + p}")

                stageA(0)
                stageA2(0)
                stageA(1)
                stageA2(1)
                for j in range(16):
                    if j + 2 < 16:
                        stageA(j + 2)
                    stageB(j)
                    if j + 2 < 16:
                        stageA2(j + 2)
                    if j >= 1:
                        stageC(j - 1)
                stageC(15)
            S.barrier()


        Q = contextlib.ExitStack()
        xT2 = sb("xT2", [128, 8, NT], F32, Q)
        PM = contextlib.ExitStack()
        oT = sb("oTr", [128, 8, NT], BF16, PM)
        wo = sb("wo", [128, 8, D], BF16, PM)
        wo_v = w_o.rearrange("(dc p) f -> p dc f", p=128)
        for dch in range(8):
            S.dma("pool", wo[:, :, dch * 128:(dch + 1) * 128], wo_v[:, :, dch * 128:(dch + 1) * 128], wr=[("wo", dch)],
                  key=f"wo{dch}")
        for tt in range(4):
            ts = slice(tt * 512, (tt + 1) * 512)
            dep = [("wo", dch) for dch in range(8)] if tt else []
            S.dma("sp", oT[:, :, ts], oT_d[:, :, ts], rd=dep, wr=[("oTr", tt)], key=f"x2sp{tt}")
            S.dma("sp", xT2[:, :, ts], x1sp[:, :, ts], rd=dep, wr=[("xT", dc, tt) for dc in range(8)], key=f"x1ld{tt}")
        lnb2 = ln_bufs(PM)
        wocnt = [0]

        def before2(tt):
            ts = slice(tt * 512, (tt + 1) * 512)
            for dch in range(8):
                bank = wocnt[0] % 2
                wocnt[0] += 1
                S.pe([lambda e, c=c, dch=dch, ts=ts, bank=bank: e.matmul(
                    PS[:, bank, :], lhsT=wo[:, c, dch * 128:(dch + 1) * 128], rhs=oT[:, c, ts], start=(c == 0), stop=(c == 7))
                    for c in range(8)], rd=[("wo", dch), ("oTr", tt)], wr=[("ps", bank)])
                S.op("dve", lambda e, dch=dch, ts=ts, bank=bank: e.scalar_tensor_tensor(
                    out=xT2[:, dch, ts], in0=PS[:, bank, :], scalar=V(G2, dch), in1=xT2[:, dch, ts],
                    op0=ALU.mult, op1=ALU.add), rd=[("ps", bank), ("vec", G2), ("xT", dch, tt)], wr=[("xT", dch, tt)])
        ln_all(xT2, AG2, AB2, A3, B3, lnb2, before=before2)
        if DEBUG == "x2":
            S.dma("sp", dbg, xT2, rd=[("xT", dc, tt) for dc in range(8) for tt in range(4)], wr=["dbg"], key="dbg")
        S.barrier()
        PM.close()
        if DEBUG == "x2":
            return nc

        with contextlib.ExitStack() as P3:
            xT3 = xT2
            lnb3 = ln_bufs(P3, nrot=2)
            otok = [sb(f"otok{i}", [128, D], F32, P3) for i in range(2)]
            def after3(tt):
                for b4 in range(4):
                    blk = tt * 4 + b4
                    sl = ocnt[0] % 2
                    ocnt[0] += 1
                    b0 = 2 * sl
                    pv = PS[:, b0:b0 + 2, :].rearrange("p b (k c) -> p (b k) c", c=128)
                    S.pe([lambda e, dc=dc, blk=blk, pv=pv: e.transpose(out=pv[:, dc, :], in_=xT3[:, dc, blk * 128:(blk + 1) * 128],
                                                                      identity=ident_f) for dc in range(8)],
                         rd=[("xT", dc, tt) for dc in range(8)] + ["ident_f"], wr=[("ps", b0), ("ps", b0 + 1)])
                    for hb in range(2):
                        eng = "act" if hb else "dve"
                        ov = otok[sl].rearrange("p (k c) -> p k c", c=128)[:, hb * 4:hb * 4 + 4, :]
                        if hb:
                            S.op("act", lambda e, ov=ov, pv=pv: e.activation(out=ov, in_=pv[:, 4:8, :], func=AF.Copy),
                                 rd=[("ps", b0 + 1)], wr=[("otok", sl, 1)])
                        else:
                            S.op("dve", lambda e, ov=ov, pv=pv: e.tensor_copy(out=ov, in_=pv[:, 0:4, :]),
                                 rd=[("ps", b0)], wr=[("otok", sl, 0)])
                    S.dma("sp", out[blk * 128:(blk + 1) * 128, :], otok[sl], rd=[("otok", sl, 0), ("otok", sl, 1)],
                          wr=[("out", blk)], key=f"out{sl}")
            ocnt = [0]

            def ap3(tt):
                ln_apply(xT3, tt, 13, 14, None, None, lnb3)
                after3(tt)
            ffn(P3, f2g, f2u, f2d, G3H, xT3, ln_cb=(lambda tt: ln_stats(xT3, tt, lnb3), ap3))
        S.barrier()
        Q.close()
    return nc


def _host_consts(g):
    ident = np.eye(128, dtype=np.float32)
    s = np.arange(128)
    maskT = np.where(s[:, None] <= s[None, :], 0.0, NEG).astype(np.float32)
    lg = np.log1p(-np.power(np.float32(2.0), -5.0 - np.arange(RET_H, dtype=np.float32))).astype(np.float32)
    idx = np.arange(128, dtype=np.float32)
    diff = idx[None, :] - idx[:, None]
    decT = np.where(diff >= 0, np.exp(lg[:, None, None] * np.maximum(diff, 0.0)), 0.0).astype(np.float32)
    decT = (decT * np.float32(128.0 ** -0.5)).transpose(1, 0, 2).copy()
    qdec = np.exp(lg[None, :] * (idx[:, None] + 1.0)).astype(np.float32)
    kdec = (np.exp(lg[None, :] * (127.0 - idx[:, None])) * np.float32(128.0 ** -0.5)).astype(np.float32)
    cpow = np.exp(lg[None, :] * 128.0 * (15.0 - np.arange(16, dtype=np.float32))[:, None]).astype(np.float32)
    kdecJ = (kdec[:, None, :] * cpow[None, :, :]).reshape(128, 64).astype(np.float32)
    qkdec = np.concatenate([qdec, kdec, kdecJ], axis=1).astype(np.float32)
    inv_freq = (np.float32(10000.0) ** (-np.arange(64, dtype=np.float32) / np.float32(64))).astype(np.float32)

    def rope(p0):
        pos = (p0 + np.arange(2048)).astype(np.float32)
        ang = (pos[:, None] * inv_freq[None, :]).astype(np.float32)
        t = np.concatenate([np.cos(ang), np.sin(ang)], axis=1).astype(np.float32)
        return t.reshape(16, 128, 128).transpose(1, 0, 2).copy()

    return dict(ident=ident, maskT=maskT, decT=decT, qkdec=qkdec, rope_prev=rope(0), rope_own=rope(g * 2048))


def kernel(**inputs):
    f = lambda a: np.ascontiguousarray(np.asarray(a, dtype=np.float32))
    x = f(inputs["x"]); c = f(inputs["c"])

    def fm(v, n):
        return np.ascontiguousarray(f(v).reshape(n, 128).T)

    lnp = np.concatenate([fm(inputs[k][0], 8) for k in ("ln1_g", "ln1_b", "ln2_g", "ln2_b", "ln3_g", "ln3_b")], axis=1)
    gnp = np.concatenate([fm(inputs["ret_gn_g"][0], 4), fm(inputs["ret_gn_b"][0], 4)], axis=1)
    shared = {
        "w_ada": f(inputs["w_ada"][0]), "b_ada": fm(inputs["b_ada"][0], 72),
        "ffn1_w_gate": f(inputs["ffn1_w_gate"][0]), "ffn1_w_up": f(inputs["ffn1_w_up"][0]),
        "ffn1_w_down": f(inputs["ffn1_w_down"][0]),
        "ffn2_w_gate": f(inputs["ffn2_w_gate"][0]), "ffn2_w_up": f(inputs["ffn2_w_up"][0]),
        "ffn2_w_down": f(inputs["ffn2_w_down"][0]),
        "lnp": np.ascontiguousarray(lnp), "w_in": f(inputs["w_in"][0]),
        "fox_b_f": f(inputs["fox_b_f"][0]).reshape(8, 1), "gnp": np.ascontiguousarray(gnp), "w_o": f(inputs["w_o"][0]),
    }
    in_maps = []
    for core in range(8):
        b, g = core // 2, core % 2
        m = dict(shared)
        m["x"] = np.ascontiguousarray(x[b, g * NT:(g + 1) * NT, :])
        m["cT"] = fm(c[b], 8)
        gs = np.zeros((128, 3), np.float32)
        gs[:, 0] = float(g)
        gs[:, 1] = NEG * (1.0 - g)
        gs[:, 2] = np.where((np.arange(128) % 16) < 8, NEG * (1.0 - g), 0.0)
        m["gsel"] = gs
        m.update(_host_consts(g))
        in_maps.append(m)
    nc = build_nc()
    res = run_bass_kernel_spmd(nc, in_maps, core_ids=list(range(8)))
    outp = np.empty((4, 4096, D), np.float32)
    for core in range(8):
        b, g = core // 2, core % 2
        outp[b, g * NT:(g + 1) * NT, :] = res.results[core]["out"]
    if DEBUG is not None:
        kernel.dbg = [res.results[core]["dbg"] for core in range(8)]
    return outp
```
